# Optimizing a Trainium2 kernel written in Bass

```python
import math
import jax
import jax.numpy as jnp
from jax import lax
import numpy as np

D_MODEL = 1024
BATCH = 4
SEQ = 4096
DEPTH = 4
DEC_BATCH = 16
DEC_SEQ = 4096
PAST_LEN = 128

GRID_W = 64
D_MIX = D_MODEL
N_MIXERS = 4
W_GROUP = D_MIX // N_MIXERS
D_FF = 4 * D_MODEL
NORM_EPS = 1e-6

S5_CH = 16
S5_GROUPS = W_GROUP // S5_CH
S5_STATE = 64
S5_IN = W_GROUP

HY_ORDER = 2
HY_BANDS = 8
HY_EMB = 1 + 2 * HY_BANDS
HY_FFN = 64
HY_IN = (HY_ORDER + 1) * W_GROUP

RW_HEAD = 64
RW_HEADS = W_GROUP // RW_HEAD
RW_DECAY_RANK = 64
RW_A_RANK = 64
RW_G_RANK = 128
RW_LN_EPS = 64e-5
RW_IN = 3 * W_GROUP + RW_DECAY_RANK + RW_A_RANK + RW_G_RANK
RW_SPLITS = (W_GROUP, 2 * W_GROUP, 3 * W_GROUP, 3 * W_GROUP + RW_DECAY_RANK, 3 * W_GROUP + RW_DECAY_RANK + RW_A_RANK)

NA_HEAD = 64
NA_HEADS = W_GROUP // NA_HEAD
NA_WIN_R = 8
NA_WIN_C = 16
NA_COL_BLOCK = 16
NA_COL_SPAN = 32
NA_IN = 3 * W_GROUP
NEG_INF = -1e30

D_IN = S5_IN + HY_IN + RW_IN + NA_IN
IN_SPLITS = (S5_IN, S5_IN + HY_IN, S5_IN + HY_IN + RW_IN)

kernel_name = 'hybrid_bidir_encoder_parallel_heads'


def rms_norm(x, g):
    xf = x.astype(jnp.float32)
    y = xf * lax.rsqrt(jnp.mean(xf * xf, axis=-1, keepdims=True) + NORM_EPS)
    return y * g.astype(jnp.float32)


def shift_prev(x):
    return jnp.pad(x[:, :-1], ((0, 0), (1, 0), (0, 0)))


def shift_next(x):
    return jnp.pad(x[:, 1:], ((0, 0), (0, 1), (0, 0)))


def cmul(ar, ai, br, bi):
    return ar * br - ai * bi, ar * bi + ai * br


def s5_scan(u, lam_re, lam_im, log_dt, b_re, b_im, c_re, c_im, reverse):
    f32 = jnp.float32
    lam_re = lam_re.astype(f32)
    lam_im = lam_im.astype(f32)
    dt = jnp.exp(log_dt.astype(f32))[:, None]
    mag = jnp.exp(lam_re * dt)
    ab_re = mag * jnp.cos(lam_im * dt)
    ab_im = mag * jnp.sin(lam_im * dt)
    den = lam_re * lam_re + lam_im * lam_im
    n_re = ab_re - 1.0
    f_re = (n_re * lam_re + ab_im * lam_im) / den
    f_im = (ab_im * lam_re - n_re * lam_im) / den
    bb_re, bb_im = cmul(f_re[..., None], f_im[..., None], b_re.astype(f32), b_im.astype(f32))
    bu_re = jnp.einsum('blgc,gnc->blgn', u, bb_re)
    bu_im = jnp.einsum('blgc,gnc->blgn', u, bb_im)
    a_re = jnp.broadcast_to(ab_re, bu_re.shape)
    a_im = jnp.broadcast_to(ab_im, bu_im.shape)

    def combine(e1, e2):
        a1r, a1i, b1r, b1i = e1
        a2r, a2i, b2r, b2i = e2
        ar, ai = cmul(a2r, a2i, a1r, a1i)
        br, bi = cmul(a2r, a2i, b1r, b1i)
        return ar, ai, br + b2r, bi + b2i

    _, _, s_re, s_im = lax.associative_scan(combine, (a_re, a_im, bu_re, bu_im), reverse=reverse, axis=1)
    return (jnp.einsum('blgn,gcn->blgc', s_re, c_re.astype(f32))
            - jnp.einsum('blgn,gcn->blgc', s_im, c_im.astype(f32)))


def s5_mixer(z, lam_re, lam_im, log_dt, b_re, b_im, c_re, c_im, d, glu_w, glu_b):
    z = z.astype(jnp.float32)
    Bn, L, _ = z.shape
    u = z.reshape(Bn, L, S5_GROUPS, S5_CH)
    y_f = s5_scan(u, lam_re[0], lam_im[0], log_dt[0], b_re[0], b_im[0], c_re[0], c_im[0], False)
    y_b = s5_scan(u, lam_re[1], lam_im[1], log_dt[1], b_re[1], b_im[1], c_re[1], c_im[1], True)
    y = (y_f + y_b).reshape(Bn, L, W_GROUP) + d * z
    g = jax.nn.gelu(y)
    return g * jax.nn.sigmoid(g @ glu_w + glu_b)


def hyena_filter_spectra(L, w1, b1, freq, w2, b2, w3, log_rate):
    t = jnp.arange(L, dtype=jnp.float32) / L
    ang = 2.0 * math.pi * t[:, None] * jnp.arange(1, HY_BANDS + 1, dtype=jnp.float32)
    feats = jnp.concatenate([t[:, None], jnp.sin(ang), jnp.cos(ang)], axis=-1)
    h = jnp.sin(freq[0] * (feats @ w1 + b1))
    h = jnp.sin(freq[1] * (h @ w2 + b2))
    h = (h @ w3).reshape(L, 2, HY_ORDER, W_GROUP)
    h = h * jnp.exp(-jnp.exp(log_rate.astype(jnp.float32))[None] * t[:, None, None, None])
    fwd, bwd = h[:, 0], h[:, 1]
    k = jnp.concatenate([fwd, jnp.zeros_like(fwd[:1]), bwd[:0:-1]], axis=0)
    k = k / jnp.sum(jnp.abs(k), axis=0, keepdims=True)
    return jnp.fft.rfft(k, axis=0)


def hyena_mixer(z, conv_w, conv_b, f_w1, f_b1, f_freq, f_w2, f_b2, f_w3, log_rate, d):
    z = z.astype(jnp.float32)
    Bn, L, _ = z.shape
    z = conv_w[0] * shift_prev(z) + conv_w[1] * z + conv_w[2] * shift_next(z) + conv_b
    v, x1, x2 = jnp.split(z, HY_ORDER + 1, axis=-1)
    kf = hyena_filter_spectra(L, f_w1, f_b1, f_freq, f_w2, f_b2, f_w3, log_rate)
    u = v
    for o, gate in enumerate((x1, x2)):
        conv = jnp.fft.irfft(jnp.fft.rfft(u, n=2 * L, axis=1) * kf[None, :, o], n=2 * L, axis=1)[:, :L]
        u = gate * (conv + d[o] * u)
    return u


def wkv7_scan(r, w, k, v, a, b, reverse):
    Bn, L, H, N = r.shape
    seq = tuple(jnp.moveaxis(t, 1, 0) for t in (r, w, k, v, a, b))

    def step(S, inp):
        rt, wt, kt, vt, at, bt = inp
        sa = jnp.einsum('bhvk,bhk->bhv', S, at)
        S = S * wt[:, :, None, :] + sa[..., None] * bt[:, :, None, :] + vt[..., None] * kt[:, :, None, :]
        return S, jnp.einsum('bhvk,bhk->bhv', S, rt)

    S0 = jnp.zeros((Bn, H, N, N), jnp.float32)
    _, y = lax.scan(step, S0, seq, reverse=reverse)
    return jnp.moveaxis(y, 0, 1)


def rwkv_mixer(z, mu, w0, w2, a0, a2, g2, k_k, k_a, r_k, ln_w, ln_b):
    z = z.astype(jnp.float32)
    Bn, L, _ = z.shape
    z = z + mu * (0.5 * (shift_prev(z) + shift_next(z)) - z)
    r, k, v, wd, ad, gd = jnp.split(z, RW_SPLITS, axis=-1)

    def heads(t):
        return t.reshape(Bn, L, RW_HEADS, RW_HEAD)

    g = jax.nn.sigmoid(gd) @ g2
    kk = heads(k * k_k)
    kk = kk / jnp.maximum(jnp.sqrt(jnp.sum(kk * kk, axis=-1, keepdims=True)), 1e-12)
    ys = []
    for d in range(2):
        w = -jax.nn.softplus(-(w0[d] + jnp.tanh(wd) @ w2[d])) - 0.5
        decay = jnp.exp(-jnp.exp(w))
        a = jax.nn.sigmoid(a0[d] + ad @ a2[d])
        kd = k * (1.0 + (a - 1.0) * k_a)
        ys.append(wkv7_scan(heads(r), heads(decay), heads(kd), heads(v), -kk, kk * heads(a), d == 1))
    y = ys[0] + ys[1]
    mean = jnp.mean(y, axis=-1, keepdims=True)
    var = jnp.mean(jnp.square(y - mean), axis=-1, keepdims=True)
    y = ((y - mean) * lax.rsqrt(var + RW_LN_EPS)).reshape(Bn, L, W_GROUP) * ln_w + ln_b
    bonus = jnp.sum(heads(r) * heads(k) * r_k, axis=-1, keepdims=True) * heads(v)
    return (y + bonus.reshape(Bn, L, W_GROUP)) * g


def na_mixer(z, q_g, k_g, rel_bias):
    z = z.astype(jnp.float32)
    Bn, L, _ = z.shape
    rows = L // GRID_W
    wr = min(NA_WIN_R, rows)
    q, k, v = [t.reshape(Bn, rows, GRID_W, NA_HEADS, NA_HEAD) for t in jnp.split(z, 3, axis=-1)]
    q = rms_norm(q, q_g) * (NA_HEAD ** -0.5)
    k = rms_norm(k, k_g)
    n_cb = GRID_W // NA_COL_BLOCK
    qcol = np.arange(GRID_W).reshape(n_cb, NA_COL_BLOCK)
    span0 = np.clip(qcol[:, 0] - NA_WIN_C // 2, 0, GRID_W - NA_COL_SPAN)
    kcol = span0[:, None] + np.arange(NA_COL_SPAN)[None, :]
    wstart = np.clip(qcol - NA_WIN_C // 2, 0, GRID_W - NA_WIN_C)
    col_mask = (kcol[:, None, :] >= wstart[:, :, None]) & (kcol[:, None, :] < wstart[:, :, None] + NA_WIN_C)
    dc_idx = np.clip(kcol[:, None, :] - qcol[:, :, None] + NA_WIN_C - 1, 0, 2 * NA_WIN_C - 2)
    k_sp = k[:, :, kcol]
    v_sp = v[:, :, kcol]
    q_rows = jnp.moveaxis(q.reshape(Bn, rows, n_cb, NA_COL_BLOCK, NA_HEADS, NA_HEAD), 1, 0)
    bias_tab = rel_bias.astype(jnp.float32)
    mask = jnp.asarray(col_mask)[None, None, :, :, None, :]

    def row_block(args):
        r, q_r = args
        rs = jnp.clip(r - wr // 2, 0, rows - wr)
        k_r = lax.dynamic_slice_in_dim(k_sp, rs, wr, axis=1)
        v_r = lax.dynamic_slice_in_dim(v_sp, rs, wr, axis=1)
        s = jnp.einsum('bcqhd,bwckhd->bhcqwk', q_r, k_r)
        dr_idx = rs + jnp.arange(wr) - r + (NA_WIN_R - 1)
        bias = jnp.transpose(bias_tab[:, dr_idx][:, :, dc_idx], (0, 2, 3, 1, 4))
        s = jnp.where(mask, s + bias[None], NEG_INF)
        shp = s.shape
        pr = jax.nn.softmax(s.reshape(shp[:4] + (wr * NA_COL_SPAN,)), axis=-1).reshape(shp)
        return jnp.einsum('bhcqwk,bwckhd->bcqhd', pr, v_r)

    out = lax.map(row_block, (jnp.arange(rows), q_rows))
    return jnp.moveaxis(out, 0, 1).reshape(Bn, L, W_GROUP)


def encoder_trunk(x, p):
    for l in range(DEPTH):
        h = rms_norm(x, p['ln1_g'][l]).astype(x.dtype)
        z = h @ p['w_in'][l]
        z_s5, z_hy, z_rw, z_na = jnp.split(z, IN_SPLITS, axis=-1)
        y_s5 = s5_mixer(z_s5, p['s5_lam_re'][l], p['s5_lam_im'][l], p['s5_log_dt'][l], p['s5_b_re'][l],
                        p['s5_b_im'][l], p['s5_c_re'][l], p['s5_c_im'][l], p['s5_d'][l],
                        p['s5_glu_w'][l], p['s5_glu_b'][l])
        y_hy = hyena_mixer(z_hy, p['hy_conv_w'][l], p['hy_conv_b'][l], p['hy_f_w1'][l], p['hy_f_b1'][l],
                           p['hy_f_freq'][l], p['hy_f_w2'][l], p['hy_f_b2'][l], p['hy_f_w3'][l],
                           p['hy_log_rate'][l], p['hy_d'][l])
        y_rw = rwkv_mixer(z_rw, p['rw_mu'][l], p['rw_w0'][l], p['rw_w2'][l], p['rw_a0'][l], p['rw_a2'][l],
                          p['rw_g2'][l], p['rw_k_k'][l], p['rw_k_a'][l], p['rw_r_k'][l],
                          p['rw_ln_w'][l], p['rw_ln_b'][l])
        y_na = na_mixer(z_na, p['na_q_g'][l], p['na_k_g'][l], p['na_rel_bias'][l])
        groups = [rms_norm(y, p['grp_g'][l, i]).astype(x.dtype) for i, y in enumerate((y_s5, y_hy, y_rw, y_na))]
        x = x + jnp.concatenate(groups, axis=-1) @ p['w_out'][l]
        h = rms_norm(x, p['ln2_g'][l]).astype(x.dtype)
        x = x + jnp.square(jax.nn.relu(h @ p['w_mlp1'][l])) @ p['w_mlp2'][l]
    return x


def setup_inputs(seed: int = 0) -> dict:
    key = jax.random.key(seed)
    ks = iter(jax.random.split(key, 64))
    L = DEPTH

    def nrm(shape, scale=1.0):
        return jax.random.normal(next(ks), shape, jnp.float32) * scale

    def unif(shape, lo, hi):
        return jax.random.uniform(next(ks), shape, jnp.float32, lo, hi)

    inp = {}
    inp['x_prompt'] = nrm((BATCH, SEQ, D_MODEL))
    inp['x_sample'] = nrm((DEC_BATCH, DEC_SEQ, D_MODEL))
    inp['ln1_g'] = 1.0 + nrm((L, D_MODEL), 0.02)
    inp['w_in'] = nrm((L, D_MODEL, D_IN), D_MODEL ** -0.5)
    inp['s5_lam_re'] = -0.5 + nrm((L, 2, S5_GROUPS, S5_STATE), 0.01)
    inp['s5_lam_im'] = math.pi * jnp.arange(S5_STATE, dtype=jnp.float32) + nrm((L, 2, S5_GROUPS, S5_STATE), 0.01)
    inp['s5_log_dt'] = unif((L, 2, S5_GROUPS), math.log(1e-3), math.log(1e-1))
    inp['s5_b_re'] = nrm((L, 2, S5_GROUPS, S5_STATE, S5_CH), (2 * S5_CH) ** -0.5)
    inp['s5_b_im'] = nrm((L, 2, S5_GROUPS, S5_STATE, S5_CH), (2 * S5_CH) ** -0.5)
    inp['s5_c_re'] = nrm((L, 2, S5_GROUPS, S5_CH, S5_STATE), S5_STATE ** -0.5)
    inp['s5_c_im'] = nrm((L, 2, S5_GROUPS, S5_CH, S5_STATE), S5_STATE ** -0.5)
    inp['s5_d'] = nrm((L, W_GROUP))
    inp['s5_glu_w'] = nrm((L, W_GROUP, W_GROUP), W_GROUP ** -0.5)
    inp['s5_glu_b'] = nrm((L, W_GROUP), 0.02)
    inp['hy_conv_w'] = nrm((L, 3, HY_IN), 3 ** -0.5)
    inp['hy_conv_b'] = nrm((L, HY_IN), 0.02)
    inp['hy_f_w1'] = nrm((L, HY_EMB, HY_FFN), HY_EMB ** -0.5)
    inp['hy_f_b1'] = nrm((L, HY_FFN), 0.1)
    inp['hy_f_freq'] = 1.0 + nrm((L, 2, HY_FFN), 0.1)
    inp['hy_f_w2'] = nrm((L, HY_FFN, HY_FFN), HY_FFN ** -0.5)
    inp['hy_f_b2'] = nrm((L, HY_FFN), 0.1)
    inp['hy_f_w3'] = nrm((L, HY_FFN, 2 * HY_ORDER * W_GROUP), HY_FFN ** -0.5)
    inp['hy_log_rate'] = unif((L, 2, HY_ORDER, W_GROUP), math.log(3.0), math.log(15.0))
    inp['hy_d'] = nrm((L, HY_ORDER, W_GROUP))
    inp['rw_mu'] = unif((L, RW_IN), 0.2, 0.8)
    inp['rw_w0'] = jnp.linspace(-6.5, -1.0, W_GROUP, dtype=jnp.float32) + nrm((L, 2, W_GROUP), 0.1)
    inp['rw_w2'] = nrm((L, 2, RW_DECAY_RANK, W_GROUP), 0.5 * RW_DECAY_RANK ** -0.5)
    inp['rw_a0'] = nrm((L, 2, W_GROUP), 0.1)
    inp['rw_a2'] = nrm((L, 2, RW_A_RANK, W_GROUP), 0.5 * RW_A_RANK ** -0.5)
    inp['rw_g2'] = nrm((L, RW_G_RANK, W_GROUP), RW_G_RANK ** -0.5)
    inp['rw_k_k'] = 0.85 + nrm((L, W_GROUP), 0.02)
    inp['rw_k_a'] = 1.0 + nrm((L, W_GROUP), 0.02)
    inp['rw_r_k'] = nrm((L, RW_HEADS, RW_HEAD), 0.1)
    inp['rw_ln_w'] = 1.0 + nrm((L, W_GROUP), 0.02)
    inp['rw_ln_b'] = nrm((L, W_GROUP), 0.02)
    inp['na_q_g'] = 1.0 + nrm((L, NA_HEAD), 0.02)
    inp['na_k_g'] = 1.0 + nrm((L, NA_HEAD), 0.02)
    inp['na_rel_bias'] = nrm((L, NA_HEADS, 2 * NA_WIN_R - 1, 2 * NA_WIN_C - 1), 0.1)
    inp['grp_g'] = 1.0 + nrm((L, N_MIXERS, W_GROUP), 0.02)
    inp['w_out'] = nrm((L, D_MIX, D_MODEL), D_MIX ** -0.5)
    inp['ln2_g'] = 1.0 + nrm((L, D_MODEL), 0.02)
    inp['w_mlp1'] = nrm((L, D_MODEL, D_FF), D_MODEL ** -0.5)
    inp['w_mlp2'] = nrm((L, D_FF, D_MODEL), D_FF ** -0.5)
    return inp


def reference(x_prompt, x_sample, ln1_g, w_in, s5_lam_re, s5_lam_im, s5_log_dt, s5_b_re, s5_b_im,
              s5_c_re, s5_c_im, s5_d, s5_glu_w, s5_glu_b, hy_conv_w, hy_conv_b, hy_f_w1, hy_f_b1,
              hy_f_freq, hy_f_w2, hy_f_b2, hy_f_w3, hy_log_rate, hy_d, rw_mu, rw_w0, rw_w2, rw_a0,
              rw_a2, rw_g2, rw_k_k, rw_k_a, rw_r_k, rw_ln_w, rw_ln_b, na_q_g, na_k_g, na_rel_bias,
              grp_g, w_out, ln2_g, w_mlp1, w_mlp2):
    p = dict(ln1_g=ln1_g, w_in=w_in, s5_lam_re=s5_lam_re, s5_lam_im=s5_lam_im, s5_log_dt=s5_log_dt,
             s5_b_re=s5_b_re, s5_b_im=s5_b_im, s5_c_re=s5_c_re, s5_c_im=s5_c_im, s5_d=s5_d,
             s5_glu_w=s5_glu_w, s5_glu_b=s5_glu_b, hy_conv_w=hy_conv_w, hy_conv_b=hy_conv_b,
             hy_f_w1=hy_f_w1, hy_f_b1=hy_f_b1, hy_f_freq=hy_f_freq, hy_f_w2=hy_f_w2, hy_f_b2=hy_f_b2,
             hy_f_w3=hy_f_w3, hy_log_rate=hy_log_rate, hy_d=hy_d, rw_mu=rw_mu, rw_w0=rw_w0, rw_w2=rw_w2,
             rw_a0=rw_a0, rw_a2=rw_a2, rw_g2=rw_g2, rw_k_k=rw_k_k, rw_k_a=rw_k_a, rw_r_k=rw_r_k,
             rw_ln_w=rw_ln_w, rw_ln_b=rw_ln_b, na_q_g=na_q_g, na_k_g=na_k_g, na_rel_bias=na_rel_bias,
             grp_g=grp_g, w_out=w_out, ln2_g=ln2_g, w_mlp1=w_mlp1, w_mlp2=w_mlp2)
    y_prompt = encoder_trunk(x_prompt, p)
    y_sample = encoder_trunk(x_sample, p)
    return (y_prompt, y_sample)
```

```python
import math
from contextlib import ExitStack
import numpy as np
import concourse.bass as bass
import concourse.mybir as mybir
from concourse.ap import AP
from concourse.bass_utils import run_bass_kernel_spmd

F32 = mybir.dt.float32
BF16 = mybir.dt.bfloat16
I32 = mybir.dt.int32
AF = mybir.ActivationFunctionType
ALU = mybir.AluOpType
AX = mybir.AxisListType

D = 1024
L = 4096
DIN = 2816
DFF = 4096
DEPTH = 4
NCORES = 8
EPS = 1e-6
PI = math.pi


class Sched:
    NDMA = 16
    EAGER = False
    SEMPAD = 0

    def __init__(self, nc, stack):
        import bisect
        self._bisect = bisect
        self.nc = nc
        self.eng = {'pe': nc.tensor, 'dve': nc.vector, 'act': nc.scalar, 'pool': nc.gpsimd, 'sp': nc.sync}
        self.sem = {}
        self._pad = [stack.enter_context(nc.semaphore("pad%d" % i)) for i in range(self.SEMPAD)]
        for e in self.eng:
            self.sem[e] = stack.enter_context(nc.semaphore("s_" + e))
        self.dq = {}
        for q in ('sp', 'act', 'pool'):
            self.dq[q] = [stack.enter_context(nc.semaphore("d_%s%d" % (q, i))) for i in range(self.NDMA)]
        self.dcount = {q: 0 for q in self.dq}
        self.count = {e: 0 for e in self.eng}
        self.seen = {e: {} for e in self.eng}
        self.lastw = {}
        self.readers = {}
        self.ninstr = 0
        self.last_handle = {e: None for e in self.eng}
        self.sig_idx = {e: [] for e in self.eng}
        self.nsig = 0

    def _semobj(self, sk):
        if isinstance(sk, tuple):
            return self.dq[sk[0]][sk[1]]
        return self.sem[sk]

    def _resolve(self, ev):
        sk, idx = ev
        if isinstance(sk, tuple):
            return sk, idx
        lst = self.sig_idx[sk]
        p = self._bisect.bisect_left(lst, idx)
        if p == len(lst):
            self.last_handle[sk].then_inc(self.sem[sk], 1)
            lst.append(self.count[sk])
            self.nsig += 1
            p = len(lst) - 1
        return sk, p + 1

    def _wait(self, e, ev):
        sk, v = self._resolve(ev)
        if self.seen[e].get(sk, 0) >= v:
            return
        self.eng[e].wait_ge(self._semobj(sk), v)
        self.seen[e][sk] = v
        self.ninstr += 1

    def _deps(self, e, reads, writes, is_dma=False):
        deps = []
        for k in reads:
            if k in self.lastw:
                deps.append(self.lastw[k])
        for k in writes:
            if k in self.lastw:
                ev = self.lastw[k]
                if not (e == 'pe' and ev[0] == 'pe' and not is_dma):
                    deps.append(ev)
            for ev in self.readers.get(k, ()):
                if not (e == 'pe' and ev[0] == 'pe' and not is_dma):
                    deps.append(ev)
        for ev in deps:
            self._wait(e, ev)

    def _record(self, ev, reads, writes):
        for k in reads:
            lst = self.readers.setdefault(k, [])
            lst[:] = [x for x in lst if x[0] != ev[0]]
            lst.append(ev)
        for k in writes:
            self.lastw[k] = ev
            self.readers[k] = []

    def op(self, e, fn, reads=(), writes=()):
        self._deps(e, reads, writes)
        inst = fn(self.eng[e])
        self.count[e] += 1
        self.last_handle[e] = inst
        if self.EAGER:
            inst.then_inc(self.sem[e], 1)
            self.sig_idx[e].append(self.count[e])
            self.nsig += 1
        self.ninstr += 1
        self._record((e, self.count[e]), reads, writes)

    def dma(self, q, out, in_, reads=(), writes=(), **kw):
        i = self.dcount[q]
        self.dcount[q] += 1
        slot = i % self.NDMA
        rnd = i // self.NDMA
        sk = (q, slot)
        if rnd > 0:
            self._wait(q, (sk, 16 * rnd))
        self._deps(q, reads, writes, is_dma=True)
        self.eng[q].dma_start(out=out, in_=in_, **kw).then_inc(self.dq[q][slot], 16)
        self.ninstr += 1
        self._record((sk, 16 * (rnd + 1)), reads, writes)

    def dma_cols(self, q, dst, tensor, off, pstride, npart, cstride, ncols, key):
        for c in range(ncols):
            self.dma(q, dst[:, c:c + 1], AP(tensor, off + c * cstride, [[pstride, npart], [1, 1]]), writes=[key])

    def _all_events(self):
        evs = []
        for q in self.dq:
            n = self.dcount[q]
            for slot in range(min(n, self.NDMA)):
                rounds = (n - slot + self.NDMA - 1) // self.NDMA
                evs.append(((q, slot), 16 * rounds))
        for x in self.eng:
            if self.count[x] > 0:
                evs.append((x, self.count[x]))
        return evs

    def barrier(self):
        evs = self._all_events()
        for e in self.eng:
            for ev in evs:
                self._wait(e, ev)
        self.lastw = {}
        self.readers = {}

    def finish(self, e='sp'):
        for ev in self._all_events():
            self._wait(e, ev)


_UNIQ = [0]


def sbt(nc, name, shape, dt):
    _UNIQ[0] += 1
    return nc.sbuf_tensor("%s_u%d" % (name, _UNIQ[0]), shape, dt)


def pst(nc, name, shape, dt):
    _UNIQ[0] += 1
    return nc.psum_tensor("%s_u%d" % (name, _UNIQ[0]), shape, dt)


WNAMES = ['ln1_g', 'w_in', 's5_lam_re', 's5_lam_im', 's5_log_dt', 's5_b_re', 's5_b_im',
          's5_c_re', 's5_c_im', 's5_d', 's5_glu_w', 's5_glu_b', 'hy_conv_w', 'hy_conv_b', 'hy_f_w1', 'hy_f_b1',
          'hy_f_freq', 'hy_f_w2', 'hy_f_b2', 'hy_f_w3', 'hy_log_rate', 'hy_d', 'rw_mu', 'rw_w0', 'rw_w2', 'rw_a0',
          'rw_a2', 'rw_g2', 'rw_k_k', 'rw_k_a', 'rw_r_k', 'rw_ln_w', 'rw_ln_b', 'na_q_g', 'na_k_g', 'na_rel_bias',
          'grp_g', 'w_out', 'ln2_g', 'w_mlp1', 'w_mlp2']


def build(wshapes, NL=DEPTH, NS=3, mixers=('s5', 'hy', 'rw', 'na'), dbg=()):
    T = NS * L
    nc = bass.Bass('TRN2', target_bir_lowering=False)
    xin = nc.dram_tensor("x", [T, D], F32, kind="ExternalInput")
    W = {n: nc.dram_tensor(n, list(wshapes[n]), F32, kind="ExternalInput") for n in WNAMES}
    yout = nc.dram_tensor("y", [T, D], F32, kind="ExternalOutput")
    xres = nc.dram_tensor("xres", [T, D], F32)
    xmid = nc.dram_tensor("xmid", [T, D], F32)
    zT = nc.dram_tensor("zT", [DIN, T], F32)
    yT = nc.dram_tensor("yT", [D, T], F32)
    rbpad = nc.dram_tensor("rbpad", [4 * RBH * RBW], F32)
    statics = {'m01': nc.dram_tensor("na_m01", [128, 18, 256], F32, kind="ExternalInput"),
               'mneg': nc.dram_tensor("na_mneg", [128, 18, 256], F32, kind="ExternalInput")}
    KKs = nc.dram_tensor("KKs", [2 * 256 * KW], BF16)
    statics['hy_mul'] = nc.dram_tensor("hy_mul", [17, 1], F32, kind="ExternalInput")
    statics['hy_ph'] = nc.dram_tensor("hy_ph", [17, 1], F32, kind="ExternalInput")
    NF_ = NS * 4
    RWS = nc.dram_tensor("RWS", [2 * NF_ * L * 320], F32)
    VSd = nc.dram_tensor("VSd", [128 * NF_ * L], F32)
    YSd = nc.dram_tensor("YSd", [128 * NF_ * L], F32)
    dbg_out = {}
    if 'zT' in dbg:
        dbg_out['zT'] = nc.dram_tensor("dbg_zT", [DIN, T], F32, kind="ExternalOutput")
    if 'RWS' in dbg:
        dbg_out['RWS'] = nc.dram_tensor("dbg_RWS", [2 * NF_ * L * 320], F32, kind="ExternalOutput")
        dbg_out['VSd'] = nc.dram_tensor("dbg_VSd", [128 * NF_ * L], F32, kind="ExternalOutput")
    if 'KKs' in dbg:
        dbg_out['KKs'] = nc.dram_tensor("dbg_KKs", [512 * KW], BF16, kind="ExternalOutput")
    if 'yT' in dbg:
        dbg_out['yT'] = nc.dram_tensor("dbg_yT", [D, T], F32, kind="ExternalOutput")

    def dsl(t, off, pat):
        return AP(t, off, pat)

    with ExitStack() as top:
        S = Sched(nc, top)
        ident_b = top.enter_context(sbt(nc, "ident_b", [128, 128], BF16))
        ident_f = top.enter_context(sbt(nc, "ident_f", [128, 128], F32))
        ones_f = top.enter_context(sbt(nc, "ones_f", [128, 128], F32))
        epsc = top.enter_context(sbt(nc, "epsc", [128, 1], F32))
        S.op('pool', lambda e: e.memset(ones_f[:], 1.0), writes=['ones_f'])
        S.op('pool', lambda e: e.memset(epsc[:], EPS), writes=['epsc'])
        S.op('pool', lambda e: e.affine_select(out=ident_f[:], in_=ones_f[:], pattern=[[-1, 128]],
                                               compare_op=ALU.is_equal, fill=0.0, base=0, channel_multiplier=1),
             reads=['ones_f'], writes=['ident_f'])
        S.op('dve', lambda e: e.tensor_copy(out=ident_b[:], in_=ident_f[:]), reads=['ident_f'], writes=['ident_b'])

        gstg = [top.enter_context(sbt(nc, "gstg%d" % i, [128, 2048], F32)) for i in range(2)]
        negpi = top.enter_context(sbt(nc, "negpi", [128, 1], F32))
        zero1 = top.enter_context(sbt(nc, "zero1", [128, 1], F32))
        S.op('pool', lambda e: e.memset(negpi[:], -PI), writes=['negpi'])
        S.op('pool', lambda e: e.memset(zero1[:], 0.0), writes=['zero1'])
        CONST = dict(ident_f=ident_f, ident_b=ident_b, ones_f=ones_f, gstg=gstg, negpi=negpi, zero1=zero1, epsc=epsc)
        rr = [0]

        def ev_engine():
            rr[0] += 1
            return ('dve', 'act')[rr[0] % 2]

        def copy_on(e, out, in_, reads, writes):
            if e == 'act':
                S.op('act', lambda g: g.activation(out=out, in_=in_, func=AF.Copy), reads, writes)
            else:
                S.op(e, lambda g: g.tensor_copy(out=out, in_=in_), reads, writes)

        def load_weight_bf16(st, name, wt, l, kchunks, ncols, tag):
            WB = st.enter_context(sbt(nc, name, [128, kchunks, ncols], BF16))
            CH = 2048
            stg = gstg
            i = 0
            for kc in range(kchunks):
                for c0 in range(0, ncols, CH):
                    cw = min(CH, ncols - c0)
                    b = i % 2
                    S.dma(('sp', 'act')[i % 2], stg[b][:, 0:cw], wt.ap()[l, kc * 128:(kc + 1) * 128, c0:c0 + cw],
                          writes=[('stg', b)])
                    copy_on(('dve', 'pool', 'act')[i % 3], WB[:, kc, c0:c0 + cw], stg[b][:, 0:cw],
                            reads=[('stg', b)], writes=[(tag, kc)])
                    i += 1
            return WB

        def rmsnorm_to_hT(st_tiles, xt_ap, xkey, gB, hT, hT_col0, hkey, PT, ptkey, uid):
            sq, ss, rs, hb = st_tiles
            S.op('pool', lambda e: e.tensor_tensor(out=sq[:], in0=xt_ap, in1=xt_ap, op=ALU.mult), reads=[xkey], writes=['sq' + uid])
            S.op('dve', lambda e: e.tensor_reduce(out=ss[:], in_=sq[:], axis=AX.X, op=ALU.add), reads=['sq' + uid], writes=['ss' + uid])
            S.op('act', lambda e: e.activation(out=rs[:], in_=ss[:], func=AF.Sqrt, scale=1.0 / D, bias=epsc[:, 0:1]),
                 reads=['ss' + uid, 'epsc'], writes=['rs' + uid])
            S.op('dve', lambda e: e.reciprocal(out=rs[:], in_=rs[:]), reads=['rs' + uid], writes=['rs' + uid])
            S.op('dve', lambda e: e.scalar_tensor_tensor(out=hb[:], in0=xt_ap, scalar=rs[:, 0:1], in1=gB[:],
                                                         op0=ALU.mult, op1=ALU.mult),
                 reads=[xkey, 'rs' + uid, 'gB'], writes=['hb' + uid])
            for kc in range(8):
                S.op('pe', lambda e, kc=kc: e.transpose(out=PT[:, kc, :], in_=hb[:, kc * 128:(kc + 1) * 128], identity=ident_b[:]),
                     reads=['hb' + uid, 'ident_b'], writes=[ptkey])
            copy_on(ev_engine(), hT[:, :, hT_col0:hT_col0 + 128], PT[:, :, :], reads=[ptkey], writes=[hkey])

        for l in range(NL):
            xsrc = xin if l == 0 else xres
            xdst = yout if l == NL - 1 else xres
            with ExitStack() as st:
                WinB = load_weight_bf16(st, "WinB", W['w_in'], l, 8, DIN, 'win')
                gB = st.enter_context(sbt(nc, "gB", [128, D], F32))
                S.dma('sp', gB[:], W['ln1_g'].ap()[l, :].partition_broadcast(128), writes=['gB'])
                Xb = [st.enter_context(sbt(nc, "Xb%d" % i, [128, 4, D], F32)) for i in range(2)]
                hTb = [st.enter_context(sbt(nc, "hT%d" % i, [128, 8, 512], BF16)) for i in range(2)]
                tl = [(st.enter_context(sbt(nc, "sq%d" % i, [128, D], F32)),
                       st.enter_context(sbt(nc, "ss%d" % i, [128, 1], F32)),
                       st.enter_context(sbt(nc, "rs%d" % i, [128, 1], F32)),
                       st.enter_context(sbt(nc, "hb%d" % i, [128, D], BF16))) for i in range(2)]
                zst = [st.enter_context(sbt(nc, "zst%d" % i, [128, 512], F32)) for i in range(4)]
                PT = [st.enter_context(pst(nc, "PT%d" % i, [128, 8, 128], BF16)) for i in range(2)]
                PZ = [st.enter_context(pst(nc, "PZ%d" % i, [128, 512], F32)) for i in range(4)]
                nmt = T // 512
                zi = 0
                ti = 0
                for mt in range(nmt):
                    b = mt % 2
                    tok0 = mt * 512
                    S.dma('sp', Xb[b][:], xsrc.ap()[tok0:tok0 + 512, :].rearrange("(a p) d -> p a d", p=128), writes=[('X', b)])
                    for a in range(4):
                        u = ti % 2
                        rmsnorm_to_hT(tl[u], Xb[b][:, a, :], ('X', b), gB, hTb[b], a * 128, ('hT', b), PT[u], ('PT', u), str(u))
                        ti += 1
                    for fc in range(DIN // 128):
                        pz = zi % 4
                        for kc in range(8):
                            S.op('pe', lambda e, kc=kc, fc=fc, pz=pz, b=b: e.matmul(PZ[pz][:], lhsT=WinB[:, kc, fc * 128:(fc + 1) * 128],
                                                                                   rhs=hTb[b][:, kc, :], start=(kc == 0), stop=(kc == 7)),
                                 reads=[('hT', b), ('win', kc)], writes=[('PZ', pz)])
                        copy_on(ev_engine(), zst[pz][:], PZ[pz][:], reads=[('PZ', pz)], writes=[('zst', pz)])
                        S.dma(('sp', 'pool')[zi % 2], zT.ap()[fc * 128:(fc + 1) * 128, tok0:tok0 + 512], zst[pz][:], reads=[('zst', pz)], writes=['zT'])
                        zi += 1
                S.barrier()
            if 'zT' in dbg and l == 0:
                with ExitStack() as st:
                    cp = st.enter_context(sbt(nc, "cp", [128, 2048], F32))
                    for r0 in range(0, DIN, 128):
                        for c0 in range(0, T, 2048):
                            S.dma('sp', cp[:], zT.ap()[r0:r0 + 128, c0:c0 + 2048], writes=['cp'])
                            S.dma('sp', dbg_out['zT'].ap()[r0:r0 + 128, c0:c0 + 2048], cp[:], reads=['cp'])
                    S.barrier()

            need_zero = True
            with ExitStack() as st:
                if not need_zero:
                    break_ = True
                zz = st.enter_context(sbt(nc, "zz", [128, 2048], F32))
                if need_zero:
                    S.op('pool', lambda e: e.memset(zz[:], 0.0), writes=['zz'])
                for gi, mname in enumerate(('s5', 'hy', 'rw', 'na')):
                    if mname in mixers and gi != 0:
                        continue
                    for r0 in range(gi * 256, gi * 256 + 256, 128):
                        for c0 in range(0, T, 2048):
                            S.dma('sp', yT.ap()[r0:r0 + 128, c0:c0 + 2048], zz[:], reads=['zz'], writes=['yT'])
                S.barrier()
            stop = [d_[5:] for d_ in dbg if d_.startswith('stop=')]
            stop = stop[0] if stop else ''
            if 's5' in mixers:
                phase_s5(nc, S, W, l, zT, yT, NS, CONST)
            if stop == 's5':
                continue
            if 'hy' in mixers:
                phase_hy_filters(nc, S, W, l, KKs, CONST, statics)
                if 'KKs' in dbg and l == 0:
                    with ExitStack() as st:
                        cpb = st.enter_context(sbt(nc, "cpb", [128, KW], BF16))
                        for r0 in range(0, 512, 128):
                            S.dma('sp', cpb[:], AP(KKs, r0 * KW, [[KW, 128], [1, KW]]), writes=['cpb'])
                            S.dma('sp', AP(dbg_out['KKs'], r0 * KW, [[KW, 128], [1, KW]]), cpb[:], reads=['cpb'])
                        S.barrier()
                if 'nohyconv' not in dbg:
                    phase_hy_conv(nc, S, W, l, zT, yT, KKs, NS, CONST)
            if stop == 'hyc':
                continue
            if 'rw' in mixers:
                phase_rw_prep(nc, S, W, l, zT, RWS, VSd, NS, CONST)
                if 'RWS' in dbg and l == 0:
                    with ExitStack() as st:
                        cpr = st.enter_context(sbt(nc, "cpr", [128, 2560], F32))
                        n_ = 2 * NF_ * L * 320
                        for o_ in range(0, n_, 128 * 2560):
                            S.dma('sp', cpr[:], AP(RWS, o_, [[2560, 128], [1, 2560]]), writes=['cpr'])
                            S.dma('sp', AP(dbg_out['RWS'], o_, [[2560, 128], [1, 2560]]), cpr[:], reads=['cpr'])
                        cpv = st.enter_context(sbt(nc, "cpv", [128, NF_ * L], F32))
                        S.dma('sp', cpv[:], AP(VSd, 0, [[NF_ * L, 128], [1, NF_ * L]]), writes=['cpv'])
                        S.dma('sp', AP(dbg_out['VSd'], 0, [[NF_ * L, 128], [1, NF_ * L]]), cpv[:], reads=['cpv'])
                        S.barrier()
                if stop == 'rwprep':
                    continue
                if 'norwscan' not in dbg:
                    phase_rw_scan(nc, S, RWS, VSd, YSd, NS)
                    if stop == 'rwscan':
                        continue
                    phase_rw_post(nc, S, W, l, zT, yT, YSd, NS, CONST)
            if stop == 'rwpost':
                continue
            if 'na' in mixers:
                phase_na(nc, S, W, l, zT, yT, NS, CONST, statics, rbpad)
            if 'yT' in dbg and l == 0:
                with ExitStack() as st:
                    cp = st.enter_context(sbt(nc, "cp", [128, 2048], F32))
                    for r0 in range(0, D, 128):
                        for c0 in range(0, T, 2048):
                            S.dma('sp', cp[:], yT.ap()[r0:r0 + 128, c0:c0 + 2048], writes=['cp'])
                            S.dma('sp', dbg_out['yT'].ap()[r0:r0 + 128, c0:c0 + 2048], cp[:], reads=['cp'])
                    S.barrier()

            if 'nodense' in dbg:
                continue
            with ExitStack() as st:
                WoB = load_weight_bf16(st, "WoB", W['w_out'], l, 8, D, 'wo')
                gg = st.enter_context(sbt(nc, "gg", [128, 8], F32))
                S.dma_cols('sp', gg, W['grp_g'], l * 1024, 1, 128, 128, 8, 'gg')
                Yb = [st.enter_context(sbt(nc, "Yb%d" % i, [128, 8, 512], F32)) for i in range(2)]
                Xb = [st.enter_context(sbt(nc, "Xc%d" % i, [128, 4, D], F32)) for i in range(2)]
                Xo = [st.enter_context(sbt(nc, "Xo%d" % i, [128, D], F32)) for i in range(2)]
                sq = [st.enter_context(sbt(nc, "csq%d" % i, [128, 2, 512], F32)) for i in range(2)]
                R = [st.enter_context(sbt(nc, "cR%d" % i, [128, 512], F32)) for i in range(2)]
                yn = [st.enter_context(sbt(nc, "yn%d" % i, [128, 8, 512], BF16)) for i in range(2)]
                PS = [st.enter_context(pst(nc, "PS%d" % i, [128, 512], F32)) for i in range(2)]
                PO = [st.enter_context(pst(nc, "PO%d" % i, [128, 512], F32)) for i in range(4)]
                gi = 0
                oi = 0
                for mt in range(T // 512):
                    b = mt % 2
                    tok0 = mt * 512
                    S.dma('sp', Yb[b][:], yT.ap()[:, tok0:tok0 + 512].rearrange("(k p) t -> p k t", p=128), writes=[('Y', b)])
                    S.dma('act', Xb[b][:], xsrc.ap()[tok0:tok0 + 512, :].rearrange("(a p) d -> p a d", p=128), writes=[('Xc', b)])
                    for g in range(4):
                        u = gi % 2
                        gi += 1
                        S.op('pool', lambda e, u=u, g=g, b=b: e.tensor_tensor(out=sq[u][:], in0=Yb[b][:, 2 * g:2 * g + 2, :], in1=Yb[b][:, 2 * g:2 * g + 2, :], op=ALU.mult),
                             reads=[('Y', b)], writes=[('csq', u)])
                        for c in range(2):
                            S.op('pe', lambda e, u=u, c=c: e.matmul(PS[u][:], lhsT=ones_f[:], rhs=sq[u][:, c, :], start=(c == 0), stop=(c == 1)),
                                 reads=[('csq', u), 'ones_f'], writes=[('PS', u)])
                        S.op('act', lambda e, u=u: e.activation(out=R[u][:], in_=PS[u][:], func=AF.Sqrt, scale=1.0 / 256, bias=epsc[:, 0:1]),
                             reads=[('PS', u), 'epsc'], writes=[('cR', u)])
                        S.op('dve', lambda e, u=u: e.reciprocal(out=R[u][:], in_=R[u][:]), reads=[('cR', u)], writes=[('cR', u)])
                        for c in range(2):
                            S.op('dve', lambda e, u=u, c=c, g=g, b=b: e.scalar_tensor_tensor(out=yn[b][:, 2 * g + c, :], in0=Yb[b][:, 2 * g + c, :],
                                                                                             scalar=gg[:, 2 * g + c:2 * g + c + 1], in1=R[u][:],
                                                                                             op0=ALU.mult, op1=ALU.mult),
                                 reads=[('Y', b), ('cR', u), 'gg'], writes=[('yn', b)])
                    for a in range(4):
                        xo = oi % 2
                        for h in range(2):
                            po = oi % 4 if False else (2 * (oi % 2) + h)
                            for kc in range(8):
                                S.op('pe', lambda e, kc=kc, a=a, h=h, po=po, b=b: e.matmul(PO[po][:], lhsT=yn[b][:, kc, a * 128:(a + 1) * 128],
                                                                                           rhs=WoB[:, kc, h * 512:(h + 1) * 512], start=(kc == 0), stop=(kc == 7)),
                                     reads=[('yn', b), ('wo', kc)], writes=[('PO', po)])
                            S.op('dve', lambda e, a=a, h=h, po=po, b=b, xo=xo: e.tensor_tensor(out=Xo[xo][:, h * 512:(h + 1) * 512], in0=PO[po][:],
                                                                                               in1=Xb[b][:, a, h * 512:(h + 1) * 512], op=ALU.add),
                                 reads=[('PO', po), ('Xc', b)], writes=[('Xo', xo)])
                        S.dma('pool', xmid.ap()[tok0 + a * 128:tok0 + (a + 1) * 128, :], Xo[xo][:], reads=[('Xo', xo)], writes=['xmid'])
                        oi += 1
                S.barrier()

            with ExitStack() as st:
                W1B = load_weight_bf16(st, "W1B", W['w_mlp1'], l, 8, DFF, 'w1')
                W2B = load_weight_bf16(st, "W2B", W['w_mlp2'], l, 32, D, 'w2')
                gB = st.enter_context(sbt(nc, "gB2", [128, D], F32))
                S.dma('sp', gB[:], W['ln2_g'].ap()[l, :].partition_broadcast(128), writes=['gB'])
                Xb = [st.enter_context(sbt(nc, "Xm%d" % i, [128, 2, D], F32)) for i in range(2)]
                hTb = [st.enter_context(sbt(nc, "hTm%d" % i, [128, 8, 256], BF16)) for i in range(2)]
                tl = [(st.enter_context(sbt(nc, "msq%d" % i, [128, D], F32)),
                       st.enter_context(sbt(nc, "mss%d" % i, [128, 1], F32)),
                       st.enter_context(sbt(nc, "mrs%d" % i, [128, 1], F32)),
                       st.enter_context(sbt(nc, "mhb%d" % i, [128, D], BF16))) for i in range(1)]
                Ab = st.enter_context(sbt(nc, "Ab", [128, 32, 256], BF16))
                rl = [st.enter_context(sbt(nc, "rl%d" % i, [128, 256], F32)) for i in range(2)]
                Xo = [st.enter_context(sbt(nc, "Xmo%d" % i, [128, D], F32)) for i in range(1)]
                PT = [st.enter_context(pst(nc, "PTm%d" % i, [128, 8, 128], BF16)) for i in range(2)]
                PH = [st.enter_context(pst(nc, "PH%d" % i, [128, 256], F32)) for i in range(2)]
                PO = [st.enter_context(pst(nc, "POm%d" % i, [128, 512], F32)) for i in range(4)]
                ti = 0
                hi = 0
                oi = 0
                for mt in range(T // 256):
                    b = mt % 2
                    tok0 = mt * 256
                    S.dma('sp', Xb[b][:], xmid.ap()[tok0:tok0 + 256, :].rearrange("(a p) d -> p a d", p=128), writes=[('Xm', b)])
                    for a in range(2):
                        u = ti % 2
                        rmsnorm_to_hT(tl[0], Xb[b][:, a, :], ('Xm', b), gB, hTb[b], a * 128, ('hTm', b), PT[u], ('PTm', u), 'm0')
                        ti += 1
                    for fc in range(32):
                        ph = hi % 2
                        hi += 1
                        for kc in range(8):
                            S.op('pe', lambda e, kc=kc, fc=fc, ph=ph, b=b: e.matmul(PH[ph][:], lhsT=W1B[:, kc, fc * 128:(fc + 1) * 128],
                                                                                   rhs=hTb[b][:, kc, :], start=(kc == 0), stop=(kc == 7)),
                                 reads=[('hTm', b), ('w1', kc)], writes=[('PH', ph)])
                        S.op('act', lambda e, ph=ph: e.activation(out=rl[ph][:], in_=PH[ph][:], func=AF.Relu), reads=[('PH', ph)], writes=[('rl', ph)])
                        S.op('pool', lambda e, ph=ph, fc=fc: e.tensor_tensor(out=Ab[:, fc, :], in0=rl[ph][:], in1=rl[ph][:], op=ALU.mult),
                             reads=[('rl', ph)], writes=[('Ab', fc)])
                    for a in range(2):
                        xo = 0
                        for h in range(2):
                            po = 2 * (oi % 2) + h
                            for fc in range(32):
                                S.op('pe', lambda e, fc=fc, a=a, h=h, po=po: e.matmul(PO[po][:], lhsT=Ab[:, fc, a * 128:(a + 1) * 128],
                                                                                      rhs=W2B[:, fc, h * 512:(h + 1) * 512], start=(fc == 0), stop=(fc == 31)),
                                     reads=[('Ab', fc), ('w2', fc)], writes=[('POm', po)])
                            S.op('dve', lambda e, a=a, h=h, po=po, b=b, xo=xo: e.tensor_tensor(out=Xo[xo][:, h * 512:(h + 1) * 512], in0=PO[po][:],
                                                                                               in1=Xb[b][:, a, h * 512:(h + 1) * 512], op=ALU.add),
                                 reads=[('POm', po), ('Xm', b)], writes=[('Xmo', xo)])
                        S.dma('pool', xdst.ap()[tok0 + a * 128:tok0 + (a + 1) * 128, :], Xo[xo][:], reads=[('Xmo', xo)], writes=['xdst'])
                        oi += 1
                S.barrier()
        S.finish('sp')
    print("built: instr ~", S.ninstr, "signals", S.nsig, {e: S.count[e] for e in S.count}, S.dcount)
    return nc


IMPLEMENTED = ('s5', 'hy', 'rw', 'na')

TWO_PI = 2.0 * PI


def rev(ap2d):
    return ap2d[:, ::-1]


def phase_s5(nc, S, W, l, zT, yT, NS, C):
    ident_f, ones_f, gstg, negpi, zero1 = C['ident_f'], C['ones_f'], C['gstg'], C['negpi'], C['zero1']
    with ExitStack() as st:
        def sb(name, shape, dt=F32):
            return st.enter_context(sbt(nc, name, shape, dt))

        def dve(fn, r=(), w=()):
            S.op('dve', fn, r, w)

        def pool(fn, r=(), w=()):
            S.op('pool', fn, r, w)

        def act(fn, r=(), w=()):
            S.op('act', fn, r, w)

        Tio = sb("Tio", [128, L])
        quart = sb("quart", [128, 1])
        pool(lambda e: e.memset(quart[:], 0.25), (), ['quart'])
        pool(lambda e: e.iota(Tio[:], pattern=[[1, L]], base=0, channel_multiplier=0, allow_small_or_imprecise_dtypes=True), (), ['Tio'])
        LRE = sb("LRE", [128, 16]); LIM = sb("LIM", [128, 16]); LDT = sb("LDT", [128, 16])
        S.dma_cols('sp', LRE, W['s5_lam_re'], l * 2048, 1, 128, 128, 16, 'LRE')
        S.dma_cols('act', LIM, W['s5_lam_im'], l * 2048, 1, 128, 128, 16, 'LIM')
        for gl in range(2):
            S.dma_cols('sp', LDT[gl * 64:(gl + 1) * 64, :], W['s5_log_dt'], l * 32 + gl, 0, 64, 2, 16, 'LDT')
        DT = sb("DT", [128, 16]); RHO = sb("RHO", [128, 16]); TH = sb("TH", [128, 16])
        t1 = sb("t1", [128, 16]); t2 = sb("t2", [128, 16]); SN = sb("SN", [128, 16]); CS = sb("CS", [128, 16])
        FRE = sb("FRE", [128, 16]); FIM = sb("FIM", [128, 16]); NFIM = sb("NFIM", [128, 16])
        act(lambda e: e.activation(out=DT[:], in_=LDT[:], func=AF.Exp), ['LDT'], ['DT'])
        dve(lambda e: e.tensor_tensor(out=t1[:], in0=LRE[:], in1=DT[:], op=ALU.mult), ['LRE', 'DT'], ['t1'])
        act(lambda e: e.activation(out=RHO[:], in_=t1[:], func=AF.Exp), ['t1'], ['RHO'])
        dve(lambda e: e.tensor_tensor(out=TH[:], in0=LIM[:], in1=DT[:], op=ALU.mult), ['LIM', 'DT'], ['TH'])
        KI16 = sb("KI16", [128, 16], I32); UU = sb("UU", [128, 16])
        dve(lambda e: e.tensor_scalar_mul(out=UU[:], in0=TH[:], scalar1=1.0 / TWO_PI), ['TH'], ['UU'])
        dve(lambda e: e.tensor_copy(out=KI16[:], in_=UU[:]), ['UU'], ['KI16'])
        dve(lambda e: e.tensor_tensor(out=t1[:], in0=UU[:], in1=KI16[:], op=ALU.subtract), ['UU', 'KI16'], ['t1'])
        act(lambda e: e.activation(out=SN[:], in_=t1[:], func=AF.Sin, scale=TWO_PI), ['t1'], ['SN'])
        dve(lambda e: e.tensor_scalar_add(out=UU[:], in0=UU[:], scalar1=0.25), ['UU'], ['UU'])
        dve(lambda e: e.tensor_copy(out=KI16[:], in_=UU[:]), ['UU'], ['KI16'])
        dve(lambda e: e.tensor_tensor(out=t2[:], in0=UU[:], in1=KI16[:], op=ALU.subtract), ['UU', 'KI16'], ['t2'])
        act(lambda e: e.activation(out=CS[:], in_=t2[:], func=AF.Sin, scale=TWO_PI), ['t2'], ['CS'])
        THN = sb("THN", [128, 16])
        dve(lambda e: e.tensor_scalar_mul(out=THN[:], in0=TH[:], scalar1=1.0 / TWO_PI), ['TH'], ['THN'])
        ABR = sb("ABR", [128, 16]); ABI = sb("ABI", [128, 16]); NRE = sb("NRE", [128, 16]); DEN = sb("DEN", [128, 16])
        dve(lambda e: e.tensor_tensor(out=ABR[:], in0=RHO[:], in1=CS[:], op=ALU.mult), ['RHO', 'CS'], ['ABR'])
        dve(lambda e: e.tensor_tensor(out=ABI[:], in0=RHO[:], in1=SN[:], op=ALU.mult), ['RHO', 'SN'], ['ABI'])
        dve(lambda e: e.tensor_scalar_add(out=NRE[:], in0=ABR[:], scalar1=-1.0), ['ABR'], ['NRE'])
        dve(lambda e: e.tensor_tensor(out=t1[:], in0=LRE[:], in1=LRE[:], op=ALU.mult), ['LRE'], ['t1'])
        dve(lambda e: e.tensor_tensor(out=t2[:], in0=LIM[:], in1=LIM[:], op=ALU.mult), ['LIM'], ['t2'])
        dve(lambda e: e.tensor_tensor(out=DEN[:], in0=t1[:], in1=t2[:], op=ALU.add), ['t1', 't2'], ['DEN'])
        dve(lambda e: e.reciprocal(out=DEN[:], in_=DEN[:]), ['DEN'], ['DEN'])
        dve(lambda e: e.tensor_tensor(out=t1[:], in0=NRE[:], in1=LRE[:], op=ALU.mult), ['NRE', 'LRE'], ['t1'])
        dve(lambda e: e.tensor_tensor(out=t2[:], in0=ABI[:], in1=LIM[:], op=ALU.mult), ['ABI', 'LIM'], ['t2'])
        dve(lambda e: e.tensor_tensor(out=t1[:], in0=t1[:], in1=t2[:], op=ALU.add), ['t1', 't2'], ['t1'])
        dve(lambda e: e.tensor_tensor(out=FRE[:], in0=t1[:], in1=DEN[:], op=ALU.mult), ['t1', 'DEN'], ['FRE'])
        dve(lambda e: e.tensor_tensor(out=t1[:], in0=ABI[:], in1=LRE[:], op=ALU.mult), ['ABI', 'LRE'], ['t1'])
        dve(lambda e: e.tensor_tensor(out=t2[:], in0=NRE[:], in1=LIM[:], op=ALU.mult), ['NRE', 'LIM'], ['t2'])
        dve(lambda e: e.tensor_tensor(out=t1[:], in0=t1[:], in1=t2[:], op=ALU.subtract), ['t1', 't2'], ['t1'])
        dve(lambda e: e.tensor_tensor(out=FIM[:], in0=t1[:], in1=DEN[:], op=ALU.mult), ['t1', 'DEN'], ['FIM'])
        dve(lambda e: e.tensor_scalar_mul(out=NFIM[:], in0=FIM[:], scalar1=-1.0), ['FIM'], ['NFIM'])
        MASK = sb("MASK", [128, 4, 8, 16])
        pool(lambda e: e.memset(MASK[:], 0.0), (), ['MASK'])
        for m in range(4):
            for gl in range(2):
                pool(lambda e, m=m, gl=gl: e.memset(MASK[gl * 64:(gl + 1) * 64, m, 2 * m + gl, :], 1.0), (), ['MASK'])
        BR = sb("BR", [128, 16, 16]); BI = sb("BI", [128, 16, 16])
        for dj_ in range(16):
            S.dma('sp', BR[:, dj_, :], AP(W['s5_b_re'], l * 32768 + dj_ * 2048, [[16, 128], [1, 16]]), writes=['BR'])
            S.dma('act', BI[:, dj_, :], AP(W['s5_b_im'], l * 32768 + dj_ * 2048, [[16, 128], [1, 16]]), writes=['BI'])
        BBR = sb("BBR", [128, 16, 16]); BBI = sb("BBI", [128, 16, 16]); tb16 = sb("tb16", [128, 16])
        for dj in range(16):
            dve(lambda e, dj=dj: e.tensor_scalar_mul(out=tb16[:], in0=BR[:, dj, :], scalar1=FRE[:, dj:dj + 1]), ['BR', 'FRE'], ['tb16'])
            dve(lambda e, dj=dj: e.scalar_tensor_tensor(out=BBR[:, dj, :], in0=BI[:, dj, :], scalar=NFIM[:, dj:dj + 1], in1=tb16[:], op0=ALU.mult, op1=ALU.add),
                ['BI', 'NFIM', 'tb16'], ['BBR'])
            dve(lambda e, dj=dj: e.tensor_scalar_mul(out=tb16[:], in0=BI[:, dj, :], scalar1=FRE[:, dj:dj + 1]), ['BI', 'FRE'], ['tb16'])
            dve(lambda e, dj=dj: e.scalar_tensor_tensor(out=BBI[:, dj, :], in0=BR[:, dj, :], scalar=FIM[:, dj:dj + 1], in1=tb16[:], op0=ALU.mult, op1=ALU.add),
                ['BR', 'FIM', 'tb16'], ['BBI'])
        BBT = sb("BBT", [128, 16, 2, 128], BF16)
        CET = sb("CET", [128, 16, 2, 128], BF16)
        Eb = [sb("Eb%d" % i, [128, 8, 16]) for i in range(2)]
        PTf = [st.enter_context(pst(nc, "PTf%d" % i, [128, 128], F32)) for i in range(2)]
        k = 0
        for dj in range(16):
            m = (dj % 8) % 4
            for ri, BB in enumerate((BBR, BBI)):
                u = k % 2
                k += 1
                dve(lambda e, u=u, dj=dj, m=m, BB=BB: e.tensor_tensor(out=Eb[u][:], in0=BB[:, dj:dj + 1, :].to_broadcast([128, 8, 16]), in1=MASK[:, m, :, :], op=ALU.mult),
                    ['BBR', 'BBI', 'MASK'], [('Eb', u)])
                S.op('pe', lambda e, u=u: e.transpose(out=PTf[u][:], in_=Eb[u][:].rearrange("p a b -> p (a b)"), identity=ident_f[:]),
                     [('Eb', u), 'ident_f'], [('PTf', u)])
                act(lambda e, u=u, dj=dj, ri=ri: e.activation(out=BBT[:, dj, ri, :], in_=PTf[u][:], func=AF.Copy), [('PTf', u)], ['BBT'])
        Cn = [sb("Cn%d" % i, [128, 2, 64]) for i in range(2)]
        for d in range(2):
            for h in range(2):
                for ri, nm in enumerate(('s5_c_re', 's5_c_im')):
                    u = k % 2
                    k += 1
                    off = ((l * 2 + d) * 16 + 8 * h) * 1024
                    for cp in range(2):
                        S.dma(('sp', 'act')[cp], Cn[u][:, cp, :], AP(W[nm], off, [[64, 128], [1, 64]]), writes=[('Cn', u)])
                    S.op('pe', lambda e, u=u: e.transpose(out=PTf[u][:], in_=Cn[u][:].rearrange("p a b -> p (a b)"), identity=ident_f[:]),
                         [('Cn', u), 'ident_f'], [('PTf', u)])
                    for jj in range(4):
                        dj = d * 8 + h * 4 + jj
                        sgn = 1.0 if ri == 0 else -1.0
                        dve(lambda e, u=u, dj=dj, ri=ri, jj=jj, sgn=sgn: e.scalar_tensor_tensor(
                            out=CET[:, dj, ri, :], in0=PTf[u][:], scalar=sgn, in1=MASK[:, jj, :, :].rearrange("p a b -> p (a b)"),
                            op0=ALU.mult, op1=ALU.mult), [('PTf', u), 'MASK'], ['CET'])
        GW = sb("GW", [128, 2, 256], BF16)
        for kc in range(2):
            S.dma('sp', gstg[kc][:, 0:256], W['s5_glu_w'].ap()[l, kc * 128:(kc + 1) * 128, :], writes=[('stg', kc)])
            dve(lambda e, kc=kc: e.tensor_copy(out=GW[:, kc, :], in_=gstg[kc][:, 0:256]), [('stg', kc)], ['GW'])
        DCOL = sb("DCOL", [128, 2]); GBC = sb("GBC", [128, 2])
        S.dma_cols('sp', DCOL, W['s5_d'], l * 256, 1, 128, 128, 2, 'DCOL')
        S.dma_cols('sp', GBC, W['s5_glu_b'], l * 256, 1, 128, 128, 2, 'GBC')

        COS = sb("COS", [128, L]); SIN = sb("SIN", [128, L])
        XR = sb("XR", [128, L]); XI = sb("XI", [128, L])
        SRI = sb("SRI", [128, 2, L], BF16)
        Ub = sb("Ub", [128, 2, L], BF16)
        Yacc = sb("Yacc", [128, 2, L])
        TA = [sb("TA%d" % i, [128, 512]) for i in range(2)]
        TP = [sb("TP%d" % i, [128, 512]) for i in range(2)]
        BU = [[sb("BU%d%d" % (i, r), [128, 512]) for r in range(2)] for i in range(2)]
        PB = [[st.enter_context(pst(nc, "PB%d%d" % (i, r), [128, 512], F32)) for r in range(2)] for i in range(2)]
        PY = [st.enter_context(pst(nc, "PY%d" % i, [128, 512], F32)) for i in range(2)]

        for s in range(NS):
            c0s = s * L
            i = 0
            for kc in range(2):
                for cc in range(0, L, 2048):
                    b = i % 2
                    i += 1
                    S.dma(('sp', 'act')[b], gstg[b][:], zT.ap()[kc * 128:(kc + 1) * 128, c0s + cc:c0s + cc + 2048], writes=[('stg', b)])
                    dve(lambda e, kc=kc, cc=cc, b=b: e.tensor_copy(out=Ub[:, kc, cc:cc + 2048], in_=gstg[b][:]), [('stg', b)], [('Ub', kc)])
                pool(lambda e, kc=kc: e.memset(Yacc[:, kc, :], 0.0), (), [('Yacc', kc)])
            for dj in range(16):
                d = dj // 8
                j = dj % 8
                kc = j // 4
                xkeys_r = [('XR', tb) for tb in range(8)]
                xkeys_i = [('XI', tb) for tb in range(8)]
                KIv = COS[:].bitcast(I32)
                act(lambda e, dj=dj: e.activation(out=SIN[:], in_=Tio[:], func=AF.Identity, scale=THN[:, dj:dj + 1]), ['Tio', 'THN'], ['SIN'])
                act(lambda e: e.activation(out=KIv, in_=SIN[:], func=AF.Copy), ['SIN'], ['COS'])
                dve(lambda e: e.tensor_tensor(out=XR[:], in0=SIN[:], in1=KIv, op=ALU.subtract), ['SIN', 'COS'], xkeys_r)
                act(lambda e: e.activation(out=KIv, in_=SIN[:], func=AF.Copy, bias=0.25), ['SIN', 'quart', ] + xkeys_r, ['COS'])
                dve(lambda e: e.scalar_tensor_tensor(out=XI[:], in0=SIN[:], scalar=0.25, in1=KIv, op0=ALU.add, op1=ALU.subtract), ['SIN', 'COS'], xkeys_i)
                act(lambda e: e.activation(out=SIN[:], in_=XR[:], func=AF.Sin, scale=TWO_PI), xkeys_r + xkeys_i, ['SIN'])
                act(lambda e: e.activation(out=COS[:], in_=XI[:], func=AF.Sin, scale=TWO_PI), xkeys_i, ['COS'])
                def tview(tab, c0, n, d=d):
                    if d == 0:
                        return tab[:, c0:c0 + n]
                    return rev(tab[:, L - c0 - n:L - c0])

                for tb in range(8):
                    u = tb % 2
                    c0 = tb * 512
                    for ri in range(2):
                        S.op('pe', lambda e, u=u, ri=ri, dj=dj, kc=kc, c0=c0: e.matmul(PB[u][ri][:], lhsT=BBT[:, dj, ri, :], rhs=Ub[:, kc, c0:c0 + 512], start=True, stop=True),
                             ['BBT', ('Ub', kc)], [('PB', u, ri)])
                        act(lambda e, u=u, ri=ri: e.activation(out=BU[u][ri][:], in_=PB[u][ri][:], func=AF.Copy), [('PB', u, ri)], [('BU', u, ri)])
                    cs = tview(COS, c0, 512)
                    sn = tview(SIN, c0, 512)
                    dve(lambda e, u=u, c0=c0, cs=cs: e.tensor_tensor(out=XR[:, c0:c0 + 512], in0=BU[u][0][:], in1=cs, op=ALU.mult), [('BU', u, 0), 'COS'], [('XR', tb)])
                    dve(lambda e, u=u, sn=sn: e.tensor_tensor(out=TA[0][:], in0=BU[u][1][:], in1=sn, op=ALU.mult), [('BU', u, 1), 'SIN'], [('TA', 0)])
                    dve(lambda e, c0=c0: e.tensor_tensor(out=XR[:, c0:c0 + 512], in0=XR[:, c0:c0 + 512], in1=TA[0][:], op=ALU.add), [('XR', tb), ('TA', 0)], [('XR', tb)])
                    pool(lambda e, u=u, c0=c0, cs=cs: e.tensor_tensor(out=XI[:, c0:c0 + 512], in0=BU[u][1][:], in1=cs, op=ALU.mult), [('BU', u, 1), 'COS'], [('XI', tb)])
                    pool(lambda e, u=u, sn=sn: e.tensor_tensor(out=TP[0][:], in0=BU[u][0][:], in1=sn, op=ALU.mult), [('BU', u, 0), 'SIN'], [('TP', 0)])
                    pool(lambda e, c0=c0: e.tensor_tensor(out=XI[:, c0:c0 + 512], in0=XI[:, c0:c0 + 512], in1=TP[0][:], op=ALU.subtract), [('XI', tb), ('TP', 0)], [('XI', tb)])
                xkeys_r = [('XR', tb) for tb in range(8)]
                xkeys_i = [('XI', tb) for tb in range(8)]
                rho_b = RHO[:, dj:dj + 1].to_broadcast([128, L])
                if d == 0:
                    dve(lambda e, rho_b=rho_b: e.tensor_tensor_scan(out=XR[:], data0=rho_b, data1=XR[:], initial=zero1[:, 0:1], op0=ALU.mult, op1=ALU.add),
                        xkeys_r + ['RHO', 'zero1'], xkeys_r)
                    dve(lambda e, rho_b=rho_b: e.tensor_tensor_scan(out=XI[:], data0=rho_b, data1=XI[:], initial=zero1[:, 0:1], op0=ALU.mult, op1=ALU.add),
                        xkeys_i + ['RHO', 'zero1'], xkeys_i)
                else:
                    dve(lambda e, rho_b=rho_b: e.tensor_tensor_scan(out=rev(XR[:]), data0=rho_b, data1=rev(XR[:]), initial=zero1[:, 0:1], op0=ALU.mult, op1=ALU.add),
                        xkeys_r + ['RHO', 'zero1'], xkeys_r)
                    dve(lambda e, rho_b=rho_b: e.tensor_tensor_scan(out=rev(XI[:]), data0=rho_b, data1=rev(XI[:]), initial=zero1[:, 0:1], op0=ALU.mult, op1=ALU.add),
                        xkeys_i + ['RHO', 'zero1'], xkeys_i)
                for q in range(8):
                    c0 = q * 512
                    cs = tview(COS, c0, 512)
                    sn = tview(SIN, c0, 512)
                    dve(lambda e, c0=c0, cs=cs: e.tensor_tensor(out=TA[0][:], in0=XR[:, c0:c0 + 512], in1=cs, op=ALU.mult), xkeys_r + ['COS'], [('TA', 0)])
                    dve(lambda e, c0=c0, sn=sn: e.tensor_tensor(out=TA[1][:], in0=XI[:, c0:c0 + 512], in1=sn, op=ALU.mult), xkeys_i + ['SIN'], [('TA', 1)])
                    dve(lambda e, c0=c0: e.tensor_tensor(out=SRI[:, 0, c0:c0 + 512], in0=TA[0][:], in1=TA[1][:], op=ALU.subtract), [('TA', 0), ('TA', 1)], [('SR', q)])
                    pool(lambda e, c0=c0, sn=sn: e.tensor_tensor(out=TP[0][:], in0=XR[:, c0:c0 + 512], in1=sn, op=ALU.mult), xkeys_r + ['SIN'], [('TP', 0)])
                    pool(lambda e, c0=c0, cs=cs: e.tensor_tensor(out=TP[1][:], in0=XI[:, c0:c0 + 512], in1=cs, op=ALU.mult), xkeys_i + ['COS'], [('TP', 1)])
                    pool(lambda e, c0=c0: e.tensor_tensor(out=SRI[:, 1, c0:c0 + 512], in0=TP[0][:], in1=TP[1][:], op=ALU.add), [('TP', 0), ('TP', 1)], [('SI', q)])
                for tb in range(8):
                    u = tb % 2
                    c0 = tb * 512
                    q = tb
                    S.op('pe', lambda e, u=u, dj=dj, c0=c0: e.matmul(PY[u][:], lhsT=CET[:, dj, 0, :], rhs=SRI[:, 0, c0:c0 + 512], start=True, stop=False),
                         ['CET', ('SR', q)], [('PY', u)])
                    S.op('pe', lambda e, u=u, dj=dj, c0=c0: e.matmul(PY[u][:], lhsT=CET[:, dj, 1, :], rhs=SRI[:, 1, c0:c0 + 512], start=False, stop=True),
                         ['CET', ('SI', q)], [('PY', u)])
                    dve(lambda e, u=u, kc=kc, c0=c0: e.tensor_tensor(out=Yacc[:, kc, c0:c0 + 512], in0=PY[u][:], in1=Yacc[:, kc, c0:c0 + 512], op=ALU.add),
                        [('PY', u), ('Yacc', kc)], [('Yacc', kc)])
            i = 0
            for kc in range(2):
                for q in range(8):
                    c0 = q * 512
                    b = i % 2
                    i += 1
                    S.dma(('sp', 'act')[b], gstg[b][:, 0:512], zT.ap()[kc * 128:(kc + 1) * 128, c0s + c0:c0s + c0 + 512], writes=[('stg', b)])
                    yv = Yacc[:, kc, c0:c0 + 512]
                    dve(lambda e, kc=kc, b=b, yv=yv: e.scalar_tensor_tensor(out=yv, in0=gstg[b][:, 0:512], scalar=DCOL[:, kc:kc + 1], in1=yv, op0=ALU.mult, op1=ALU.add),
                        [('stg', b), 'DCOL', ('Yacc', kc)], [('Yacc', kc)])
                    pool(lambda e, yv=yv: e.tensor_tensor(out=TP[0][:], in0=yv, in1=yv, op=ALU.mult), [('Yacc', kc)], [('TP', 0)])
                    dve(lambda e: e.tensor_scalar(out=TA[0][:], in0=TP[0][:], scalar1=0.044715, scalar2=1.0, op0=ALU.mult, op1=ALU.add), [('TP', 0)], [('TA', 0)])
                    dve(lambda e, yv=yv: e.tensor_tensor(out=TA[0][:], in0=TA[0][:], in1=yv, op=ALU.mult), [('TA', 0), ('Yacc', kc)], [('TA', 0)])
                    act(lambda e: e.activation(out=TA[1][:], in_=TA[0][:], func=AF.Sigmoid, scale=2.0 * math.sqrt(2.0 / PI)), [('TA', 0)], [('TA', 1)])
                    dve(lambda e, yv=yv: e.tensor_tensor(out=yv, in0=yv, in1=TA[1][:], op=ALU.mult), [('TA', 1), ('Yacc', kc)], [('Yacc', kc)])
                    pool(lambda e, kc=kc, c0=c0, yv=yv: e.tensor_copy(out=SRI[:, kc, c0:c0 + 512], in_=yv), [('Yacc', kc)], [(('SR', 'SI')[kc], q)])
            oi = 0
            for tb in range(8):
                c0 = tb * 512
                q = tb
                for co in range(2):
                    u = oi % 2
                    oi += 1
                    for kc in range(2):
                        S.op('pe', lambda e, u=u, kc=kc, co=co, c0=c0: e.matmul(PY[u][:], lhsT=GW[:, kc, co * 128:(co + 1) * 128], rhs=SRI[:, kc, c0:c0 + 512],
                                                                            start=(kc == 0), stop=(kc == 1)),
                             ['GW', ('SR', q), ('SI', q)], [('PY', u)])
                    act(lambda e, u=u, co=co: e.activation(out=BU[u][0][:], in_=PY[u][:], func=AF.Sigmoid, bias=GBC[:, co:co + 1]), [('PY', u), 'GBC'], [('BU', u, 0)])
                    dve(lambda e, u=u, co=co, c0=c0: e.tensor_tensor(out=BU[u][1][:], in0=BU[u][0][:], in1=Yacc[:, co, c0:c0 + 512], op=ALU.mult),
                        [('BU', u, 0), ('Yacc', co)], [('BU', u, 1)])
                    S.dma(('sp', 'pool')[u], yT.ap()[co * 128:(co + 1) * 128, c0s + c0:c0s + c0 + 512], BU[u][1][:], reads=[('BU', u, 1)], writes=['yT'])
        S.barrier()


NEGBIG = -30000.0


def na_static_masks():
    m01 = np.zeros((128, 18, 256), np.float32)
    kc = np.arange(64)
    for var, (r0, kb) in enumerate(((0, 0), (8, 4), (60, 52))):
        for kt in range(6):
            t = var * 6 + kt
            for krl in range(2):
                key_row = kb + 2 * kt + krl
                for ip in range(4):
                    r = r0 + (3 - ip)
                    rs = min(max(r - 4, 0), 56)
                    if not (rs <= key_row < rs + 8):
                        continue
                    for qp in range(64):
                        qc = 63 - qp
                        ws = min(max(qc - 8, 0), 48)
                        ok = (kc >= ws) & (kc < ws + 16)
                        m01[krl * 64 + kc[ok], t, ip * 64 + qp] = 1.0
    mneg = (1.0 - m01) * NEGBIG
    return m01, mneg.astype(np.float32)


RBW = 127
RBH = 23


def phase_na(nc, S, W, l, zT, yT, NS, C, statics, rbpad):
    ident_f, gstg = C['ident_f'], C['gstg']
    ZQ, ZK, ZV = 2048, 2304, 2560
    with ExitStack() as st:
        def sb(name, shape, dt=F32):
            return st.enter_context(sbt(nc, name, shape, dt))

        def ps(name, shape, dt=F32):
            return st.enter_context(pst(nc, name, shape, dt))

        def dve(fn, r=(), w=()):
            S.op('dve', fn, r, w)

        def pool(fn, r=(), w=()):
            S.op('pool', fn, r, w)

        def act(fn, r=(), w=()):
            S.op('act', fn, r, w)

        zr = sb("zr", [4, RBH * RBW])
        pool(lambda e: e.memset(zr[:], 0.0), (), ['zr'])
        S.dma('sp', AP(rbpad, 0, [[RBH * RBW, 4], [1, RBH * RBW]]), zr[:], reads=['zr'], writes=['rbpad'])
        S.dma('sp', AP(rbpad, 4 * RBW + 48, [[RBH * RBW, 4], [RBW, 15], [1, 31]]), W['na_rel_bias'].ap()[l], reads=['rbpad'], writes=['rbpad'])
        M01 = sb("M01", [128, 18, 256]); MNG = sb("MNG", [128, 18, 256])
        S.dma('sp', M01[:], statics['m01'].ap(), writes=['M01'])
        S.dma('act', MNG[:], statics['mneg'].ap(), writes=['MNG'])
        TB = [sb("TB%d" % i, [128, 18, 256]) for i in range(2)]
        GQ = sb("GQ", [128, 1]); GK = sb("GK", [128, 1])
        for hh in range(2):
            S.dma('sp', GQ[hh * 64:(hh + 1) * 64, :], AP(W['na_q_g'], l * 64, [[1, 64], [1, 1]]), writes=['GQ'])
            S.dma('sp', GK[hh * 64:(hh + 1) * 64, :], AP(W['na_k_g'], l * 64, [[1, 64], [1, 1]]), writes=['GK'])
        dve(lambda e: e.tensor_scalar_mul(out=GQ[:], in0=GQ[:], scalar1=0.125), ['GQ'], ['GQ'])
        BLK = sb("BLK", [128, 128])
        pool(lambda e: e.memset(BLK[:], 0.0), (), ['BLK'])
        for hh in range(2):
            pool(lambda e, hh=hh: e.memset(BLK[hh * 64:(hh + 1) * 64, hh * 64:(hh + 1) * 64], 1.0), (), ['BLK'])
        ones_b = sb("ones_b", [128, 64], BF16)
        pool(lambda e: e.memset(ones_b[:], 1.0), (), ['ones_b'])
        eps1 = C['epsc']

        QN = sb("QN", [128, 2, L], BF16); KN = sb("KN", [128, 2, L], BF16)
        VT = sb("VT", [128, 32, 256], BF16)
        OUT = [sb("OUT%d" % i, [64, L]) for i in range(2)]
        sq = [sb("nsq%d" % i, [128, 512]) for i in range(2)]
        Rr = [sb("nR%d" % i, [128, 512]) for i in range(2)]
        SC = [sb("SC%d" % i, [128, 512]) for i in range(2)]
        PTt = [sb("PTt%d" % i, [128, 6, 256], BF16) for i in range(2)]
        RD = [sb("RD%d" % i, [64, 256]) for i in range(2)]
        banks = [ps("nab%d" % i, [128, 512]) for i in range(8)]

        def bk(i):
            return ('bank', i)

        for s in range(NS):
            c0s = s * L
            ui = 0
            li = 0
            for (zoff, G, dst, nm) in ((ZQ, GQ, QN, 'QN'), (ZK, GK, KN, 'KN')):
                for kc in range(2):
                    for hcol in range(2):
                        b = li % 2
                        li += 1
                        S.dma(('sp', 'act')[b], gstg[b][:], zT.ap()[zoff + kc * 128:zoff + (kc + 1) * 128, c0s + hcol * 2048:c0s + (hcol + 1) * 2048], writes=[('stg', b)])
                        for q4 in range(4):
                            u = ui % 2
                            ui += 1
                            src = gstg[b][:, q4 * 512:(q4 + 1) * 512]
                            col = hcol * 2048 + q4 * 512
                            pool(lambda e, u=u, src=src: e.tensor_tensor(out=sq[u][:], in0=src, in1=src, op=ALU.mult), [('stg', b)], [('nsq', u)])
                            S.op('pe', lambda e, u=u: e.matmul(banks[u][:], lhsT=BLK[:], rhs=sq[u][:], start=True, stop=True), [('nsq', u), 'BLK'], [bk(u)])
                            act(lambda e, u=u: e.activation(out=Rr[u][:], in_=banks[u][:], func=AF.Sqrt, scale=1.0 / 64, bias=eps1[:, 0:1]), [bk(u)], [('nR', u)])
                            dve(lambda e, u=u: e.reciprocal(out=Rr[u][:], in_=Rr[u][:]), [('nR', u)], [('nR', u)])
                            dve(lambda e, u=u, src=src, G=G, dst=dst, kc=kc, col=col: e.scalar_tensor_tensor(out=dst[:, kc, col:col + 512], in0=src, scalar=G[:, 0:1], in1=Rr[u][:],
                                                                                                        op0=ALU.mult, op1=ALU.mult),
                                [('stg', b), ('nR', u), 'GQ', 'GK'], [(nm, kc)])
            ti = 0
            for kc in range(2):
                for hcol in range(2):
                    b = li % 2
                    li += 1
                    S.dma(('sp', 'act')[b], gstg[b][:], zT.ap()[ZV + kc * 128:ZV + (kc + 1) * 128, c0s + hcol * 2048:c0s + (hcol + 1) * 2048], writes=[('stg', b)])
                    for rtl in range(16):
                        rt = hcol * 16 + rtl
                        u = 2 + ti % 2
                        ti += 1
                        S.op('pe', lambda e, u=u, rtl=rtl, b=b: e.transpose(out=banks[u][:, 0:128], in_=gstg[b][:, rtl * 128:(rtl + 1) * 128], identity=ident_f[:]),
                             [('stg', b), 'ident_f'], [bk(u)])
                        if ti % 2:
                            act(lambda e, u=u, rt=rt, kc=kc: e.activation(out=VT[:, rt, kc * 128:(kc + 1) * 128], in_=banks[u][:, 0:128], func=AF.Copy), [bk(u)], [('VT', kc)])
                        else:
                            dve(lambda e, u=u, rt=rt, kc=kc: e.tensor_copy(out=VT[:, rt, kc * 128:(kc + 1) * 128], in_=banks[u][:, 0:128]), [bk(u)], [('VT', kc)])
            bi = 0
            for h in range(4):
                kc = h // 2
                pb = (h % 2) * 64
                ob = h % 2
                tb_ = TB[h % 2]
                k = 0
                for var in range(3):
                    rbase = (8, 4, 0)[var]
                    for kt in range(6):
                        t = var * 6 + kt
                        for krl in range(2):
                            off = h * RBH * RBW + (rbase + 2 * kt + krl) * RBW
                            S.dma(('sp', 'act', 'pool')[k % 3], tb_[krl * 64:(krl + 1) * 64, t, :].rearrange("p (a b) -> p a b", b=64),
                                  AP(rbpad, off, [[1, 64], [RBW, 4], [1, 64]]), reads=['rbpad'], writes=[('TB', h % 2)])
                            k += 1
                for half in range(2):
                    hs = slice(half * 9, half * 9 + 9)
                    dve(lambda e, tb_=tb_, hs=hs: e.tensor_tensor(out=tb_[:, hs, :], in0=tb_[:, hs, :], in1=M01[:, hs, :], op=ALU.mult), [('TB', h % 2), 'M01'], [('TB', h % 2)])
                    pool(lambda e, tb_=tb_, hs=hs: e.tensor_tensor(out=tb_[:, hs, :], in0=tb_[:, hs, :], in1=MNG[:, hs, :], op=ALU.add), [('TB', h % 2), 'MNG'], [('TB', h % 2)])
                for g in range(16):
                    r0 = 4 * g
                    var = 0 if g == 0 else (2 if g == 15 else 1)
                    kb = 0 if g == 0 else (52 if g == 15 else r0 - 4)
                    u = bi % 2
                    bi += 1
                    qv = QN[pb:pb + 64, kc, r0 * 64:r0 * 64 + 256]
                    for kp in range(3):
                        bnk = 2 + u * 3 + kp
                        for k2 in range(2):
                            kt = kp * 2 + k2
                            krow = kb + 2 * kt
                            S.op('pe', lambda e, bnk=bnk, k2=k2, krow=krow, qv=qv, pb=pb, kc=kc: e.matmul(
                                banks[bnk][:, k2 * 256:(k2 + 1) * 256], lhsT=KN[pb:pb + 64, kc, krow * 64:krow * 64 + 128], rhs=qv, start=True, stop=True),
                                [('KN', kc), ('QN', kc)], [bk(bnk)])
                        t0 = var * 6 + kp * 2
                        w = kp % 2
                        dve(lambda e, bnk=bnk, tb_=tb_, t0=t0, w=w: e.tensor_tensor(out=SC[w][:].rearrange("p (a b) -> p a b", b=256), in0=banks[bnk][:].rearrange("p (a b) -> p a b", b=256),
                                                                                    in1=tb_[:, t0:t0 + 2, ::-1], op=ALU.add),
                            [bk(bnk), ('TB', h % 2)], [('SC', w)])
                        act(lambda e, u=u, kp=kp, w=w: e.activation(out=PTt[u][:, 2 * kp:2 * kp + 2, :].rearrange("p a b -> p (a b)"), in_=SC[w][:], func=AF.Exp), [('SC', w)], [('PTt', u)])
                    PN = banks[u][0:64, 0:256]
                    PD = banks[u][0:64, 256:512]
                    for kt in range(6):
                        vt = kb // 2 + kt
                        S.op('pe', lambda e, u=u, kt=kt, vt=vt, h=h, PN=PN: e.matmul(PN, lhsT=VT[:, vt, h * 64:(h + 1) * 64], rhs=PTt[u][:, kt, :], start=(kt == 0), stop=(kt == 5)),
                             [('VT', kc), ('PTt', u)], [bk(u)])
                    for kt in range(6):
                        S.op('pe', lambda e, u=u, kt=kt, PD=PD: e.matmul(PD, lhsT=ones_b[:], rhs=PTt[u][:, kt, :], start=(kt == 0), stop=(kt == 5)),
                             ['ones_b', ('PTt', u)], [bk(u)])
                    dve(lambda e, u=u, PD=PD: e.reciprocal(out=RD[u][:], in_=PD), [bk(u)], [('RD', u)])
                    dve(lambda e, u=u, r0=r0, ob=ob, PN=PN: e.tensor_tensor(out=OUT[ob][:, r0 * 64:r0 * 64 + 256], in0=PN, in1=RD[u][:], op=ALU.mult),
                        [bk(u), ('RD', u)], [('OUT', ob)])
                S.dma('sp', yT.ap()[768 + h * 64:768 + (h + 1) * 64, c0s:c0s + L], OUT[ob][:], reads=[('OUT', ob)], writes=['yT'])
        S.barrier()


KW = 8192


def phase_hy_filters(nc, S, W, l, KKs, C, statics):
    with ExitStack() as st:
        def sb(name, shape, dt=F32):
            return st.enter_context(sbt(nc, name, shape, dt))

        def ps(name, shape, dt=F32):
            return st.enter_context(pst(nc, name, shape, dt))

        def dve(fn, r=(), w=()):
            S.op('dve', fn, r, w)

        def pool(fn, r=(), w=()):
            S.op('pool', fn, r, w)

        def act(fn, r=(), w=()):
            S.op('act', fn, r, w)

        TT = [sb("TTn", [128, L]), sb("TTr", [128, L])]
        pool(lambda e: e.iota(TT[1][:], pattern=[[1, L]], base=0, channel_multiplier=0, allow_small_or_imprecise_dtypes=True), (), ['TTr'])
        dve(lambda e: e.tensor_scalar_mul(out=TT[0][:], in0=TT[1][:], scalar1=1.0 / L), ['TTr'], ['TTn'])
        dve(lambda e: e.tensor_scalar(out=TT[1][:], in0=TT[1][:], scalar1=-1.0 / L, scalar2=(L - 1.0) / L, op0=ALU.mult, op1=ALU.add), ['TTr'], ['TTr'])
        TTk = ['TTn', 'TTr']
        W1 = sb("hW1", [17, 64]); W2 = sb("hW2", [64, 64]); W3 = sb("hW3", [64, 1024])
        S.dma('sp', W1[:], W['hy_f_w1'].ap()[l], writes=['hW1'])
        S.dma('sp', W2[:], W['hy_f_w2'].ap()[l], writes=['hW2'])
        S.dma('act', W3[:], W['hy_f_w3'].ap()[l], writes=['hW3'])
        FQ = sb("hFQ", [64, 2]); BB = sb("hBB", [64, 2]); SS = sb("hSS", [64, 2]); CC = sb("hCC", [64, 2])
        S.dma_cols('sp', FQ, W['hy_f_freq'], l * 128, 1, 64, 64, 2, 'hFQ')
        S.dma('sp', BB[:, 0:1], AP(W['hy_f_b1'], l * 64, [[1, 64], [1, 1]]), writes=['hBB'])
        S.dma('sp', BB[:, 1:2], AP(W['hy_f_b2'], l * 64, [[1, 64], [1, 1]]), writes=['hBB'])
        dve(lambda e: e.tensor_scalar_mul(out=SS[:], in0=FQ[:], scalar1=1.0 / TWO_PI), ['hFQ'], ['hSS'])
        dve(lambda e: e.tensor_tensor(out=CC[:], in0=SS[:], in1=BB[:], op=ALU.mult), ['hSS', 'hBB'], ['hCC'])
        RT = sb("hRT", [128, 8])
        S.dma_cols('sp', RT, W['hy_log_rate'], l * 1024, 1, 128, 128, 8, 'hRT')
        act(lambda e: e.activation(out=RT[:], in_=RT[:], func=AF.Exp), ['hRT'], ['hRT'])
        dve(lambda e: e.tensor_scalar_mul(out=RT[:], in0=RT[:], scalar1=-1.0), ['hRT'], ['hRT'])
        DC = sb("hDC", [128, 4])
        S.dma_cols('sp', DC, W['hy_d'], l * 512, 1, 128, 128, 4, 'hDC')
        MUL = sb("hMUL", [17, 1]); PH = sb("hPH", [17, 1])
        S.dma('sp', MUL[:], statics['hy_mul'].ap(), writes=['hMUL'])
        S.dma('sp', PH[:], statics['hy_ph'].ap(), writes=['hPH'])

        Fb = sb("hF", [17, L]); KIf = sb("hKI", [128, 512], I32)
        H1 = sb("hH1", [64, L])
        H2 = [sb("hH2n", [64, L]), sb("hH2r", [64, L])]
        Ub = [sb("hU%d" % i, [128, 512]) for i in range(2)]
        banks = [ps("hfb%d" % i, [128, 512]) for i in range(4)]
        for pi in range(2):
            tk = TTk[pi]
            for tb in range(8):
                cs = slice(tb * 512, (tb + 1) * 512)
                u = tb % 2
                dve(lambda e, u=u, cs=cs, pi=pi: e.tensor_scalar(out=Ub[u][0:17, :], in0=TT[pi][0:17, cs], scalar1=MUL[:, 0:1], scalar2=PH[:, 0:1], op0=ALU.mult, op1=ALU.add),
                    [tk, 'hMUL', 'hPH'], [('hU', u)])
                dve(lambda e, u=u: e.tensor_copy(out=KIf[0:17, :], in_=Ub[u][0:17, :]), [('hU', u)], ['hKI'])
                dve(lambda e, u=u: e.tensor_tensor(out=Ub[u][0:17, :], in0=Ub[u][0:17, :], in1=KIf[0:17, :], op=ALU.subtract), [('hU', u), 'hKI'], [('hU', u)])
                act(lambda e, u=u, cs=cs: e.activation(out=Fb[:, cs], in_=Ub[u][0:17, :], func=AF.Sin, scale=TWO_PI), [('hU', u)], ['hF'])
            dve(lambda e, pi=pi: e.tensor_copy(out=Fb[0:1, :], in_=TT[pi][0:1, :]), [tk, 'hF'], ['hF'])
            for layer_i, (Wm, kdim, src, srck, dst, dstk) in enumerate(((W1, 17, Fb, 'hF', H1, 'hH1'), (W2, 64, H1, 'hH1', H2[pi], 'hH2%d' % pi))):
                for tb in range(8):
                    cs = slice(tb * 512, (tb + 1) * 512)
                    u = tb % 2
                    S.op('pe', lambda e, u=u, cs=cs, Wm=Wm, kdim=kdim, src=src: e.matmul(banks[u][0:64, :], lhsT=Wm[0:kdim, :], rhs=src[0:kdim, cs], start=True, stop=True),
                         [srck, 'hW1', 'hW2'], [('hfb', u)])
                    act(lambda e, u=u, layer_i=layer_i: e.activation(out=Ub[u][0:64, :], in_=banks[u][0:64, :], func=AF.Identity, scale=SS[:, layer_i:layer_i + 1], bias=CC[:, layer_i:layer_i + 1]),
                        [('hfb', u), 'hSS', 'hCC'], [('hU', u)])
                    dve(lambda e, u=u: e.tensor_copy(out=KIf[0:64, :], in_=Ub[u][0:64, :]), [('hU', u)], ['hKI'])
                    dve(lambda e, u=u: e.tensor_tensor(out=Ub[u][0:64, :], in0=Ub[u][0:64, :], in1=KIf[0:64, :], op=ALU.subtract), [('hU', u), 'hKI'], [('hU', u)])
                    act(lambda e, u=u, cs=cs, dst=dst: e.activation(out=dst[:, cs], in_=Ub[u][0:64, :], func=AF.Sin, scale=TWO_PI), [('hU', u)], [dstk])
        KN = [sb("hKN0", [128, L]), sb("hKN1", [128, L])]
        KR = sb("hKR", [128, L])
        KB = [sb("hKB%d" % i, [128, L], BF16) for i in range(2)]
        Eb = [sb("hE%d" % i, [128, 512]) for i in range(2)]
        sm = sb("hsm", [128, 4])
        ei = [0]

        def evalf(dst, dstk, pi, q):
            for tb in range(8):
                cs = slice(tb * 512, (tb + 1) * 512)
                u = ei[0] % 2
                ei[0] += 1
                S.op('pe', lambda e, u=u, cs=cs, q=q, pi=pi: e.matmul(banks[2 + u][:], lhsT=W3[:, q * 128:(q + 1) * 128], rhs=H2[pi][:, cs], start=True, stop=True),
                     ['hW3', 'hH2%d' % pi], [('hfb', 2 + u)])
                act(lambda e, u=u, cs=cs, q=q, pi=pi: e.activation(out=Eb[u][:], in_=TT[pi][:, cs], func=AF.Exp, scale=RT[:, q:q + 1]), [TTk[pi], 'hRT'], [('hE', u)])
                dve(lambda e, u=u, cs=cs, dst=dst: e.tensor_tensor(out=dst[:, cs], in0=banks[2 + u][:], in1=Eb[u][:], op=ALU.mult), [('hfb', 2 + u), ('hE', u)], [dstk])

        for o in range(2):
            for half in range(2):
                q0 = o * 2 + half
                q1 = 4 + o * 2 + half
                evalf(KN[0], 'hKN0', 0, q0)
                evalf(KN[1], 'hKN1', 0, q1)
                evalf(KR, 'hKR', 1, q1 if o == 0 else q0)
                dve(lambda e: e.scalar_tensor_tensor(out=KB[0][:], in0=KN[0][:], scalar=-1.0, in1=KN[0][:], op0=ALU.mult, op1=ALU.max), ['hKN0'], [('hKB', 0)])
                dve(lambda e: e.tensor_reduce(out=sm[:, 0:1], in_=KB[0][:], axis=AX.X, op=ALU.add), [('hKB', 0)], ['hsm'])
                dve(lambda e: e.scalar_tensor_tensor(out=KB[1][:, 1:L], in0=KN[1][:, 1:L], scalar=-1.0, in1=KN[1][:, 1:L], op0=ALU.mult, op1=ALU.max), ['hKN1'], [('hKB', 1)])
                dve(lambda e: e.tensor_reduce(out=sm[:, 1:2], in_=KB[1][:, 1:L], axis=AX.X, op=ALU.add), [('hKB', 1)], ['hsm'])
                dve(lambda e: e.tensor_tensor(out=sm[:, 2:3], in0=sm[:, 0:1], in1=sm[:, 1:2], op=ALU.add), ['hsm'], ['hsm'])
                dve(lambda e: e.reciprocal(out=sm[:, 3:4], in_=sm[:, 2:3]), ['hsm'], ['hsm'])
                dcol = DC[:, o * 2 + half:o * 2 + half + 1]
                rowbase = (o * 256 + half * 128) * KW
                if o == 0:
                    dve(lambda e: e.tensor_scalar_mul(out=KN[0][:], in0=KN[0][:], scalar1=sm[:, 3:4]), ['hKN0', 'hsm'], ['hKN0'])
                    dve(lambda e, dcol=dcol: e.tensor_tensor(out=KN[0][:, 0:1], in0=KN[0][:, 0:1], in1=dcol, op=ALU.add), ['hKN0', 'hDC'], ['hKN0'])
                    act(lambda e: e.activation(out=KB[0][:], in_=KR[:], func=AF.Copy, scale=sm[:, 3:4]) if False else e.activation(out=KB[0][:], in_=KR[:], func=AF.Identity, scale=sm[:, 3:4]),
                        ['hKR', 'hsm'], [('hKB', 0)])
                    pool(lambda e: e.tensor_copy(out=KB[1][:], in_=KN[0][:]), ['hKN0'], [('hKB', 1)])
                    S.dma('sp', AP(KKs, rowbase, [[KW, 128], [1, L - 1]]), KB[0][:, 0:L - 1], reads=[('hKB', 0)], writes=['KKs'])
                    S.dma('act', AP(KKs, rowbase + L - 1, [[KW, 128], [1, L]]), KB[1][:], reads=[('hKB', 1)], writes=['KKs'])
                else:
                    dve(lambda e: e.tensor_scalar_mul(out=KR[:], in0=KR[:], scalar1=sm[:, 3:4]), ['hKR', 'hsm'], ['hKR'])
                    dve(lambda e, dcol=dcol: e.tensor_tensor(out=KR[:, L - 1:L], in0=KR[:, L - 1:L], in1=dcol, op=ALU.add), ['hKR', 'hDC'], ['hKR'])
                    pool(lambda e: e.tensor_copy(out=KB[0][:], in_=KR[:]), ['hKR'], [('hKB', 0)])
                    act(lambda e: e.activation(out=KB[1][:], in_=KN[1][:], func=AF.Identity, scale=sm[:, 3:4]), ['hKN1', 'hsm'], [('hKB', 1)])
                    S.dma('sp', AP(KKs, rowbase, [[KW, 128], [1, L]]), KB[0][:], reads=[('hKB', 0)], writes=['KKs'])
                    S.dma('act', AP(KKs, rowbase + L, [[KW, 128], [1, L - 1]]), KB[1][:, 1:L], reads=[('hKB', 1)], writes=['KKs'])
        S.barrier()


def phase_hy_conv(nc, S, W, l, zT, yT, KKs, NS, C):
    ident_f = C['ident_f']
    HB = 2048
    NA = 32
    NCOL = NA * NS
    with ExitStack() as st:
        def sb(name, shape, dt=F32):
            return st.enter_context(sbt(nc, name, shape, dt))

        def ps(name, shape, dt=F32):
            return st.enter_context(pst(nc, name, shape, dt))

        def dve(fn, r=(), w=()):
            S.op('dve', fn, r, w)

        def pool(fn, r=(), w=()):
            S.op('pool', fn, r, w)

        def act(fn, r=(), w=()):
            S.op('act', fn, r, w)

        VTM = sb("VTM", [128, NA, NS, 64], BF16)
        X1 = sb("X1TM", [128, NA, NS, 64]); X2 = sb("X2TM", [128, NA, NS, 64]); Y2 = sb("Y2TM", [128, NA, NS, 64])
        Hk = [[sb("Hk%d%d" % (i, o), [128, 8064], BF16) for o in range(2)] for i in range(2)]
        Zp = sb("Zp", [128, HB + 2]); T1 = sb("T1", [128, HB]); OA = sb("OAc", [128, HB])
        OUTF = sb("OUTF", [64, L])
        U1 = [sb("U1%d" % i, [128, NCOL], BF16) for i in range(2)]
        CWA = sb("CWA", [128, 4]); CWB = sb("CWB", [64, 4])
        banks = [ps("hcb%d" % i, [128, 512]) for i in range(8)]

        def bk(i):
            return ('hcb', i)

        ti = 0
        for cg in range(4):
            for tap in range(3):
                S.dma('sp', CWA[0:64, tap:tap + 1], AP(W['hy_conv_w'], (l * 3 + tap) * 768 + cg * 64, [[1, 64], [1, 1]]), writes=['CWA'])
                S.dma('sp', CWA[64:128, tap:tap + 1], AP(W['hy_conv_w'], (l * 3 + tap) * 768 + 512 + cg * 64, [[1, 64], [1, 1]]), writes=['CWA'])
                S.dma('act', CWB[:, tap:tap + 1], AP(W['hy_conv_w'], (l * 3 + tap) * 768 + 256 + cg * 64, [[1, 64], [1, 1]]), writes=['CWB'])
            S.dma('sp', CWA[0:64, 3:4], AP(W['hy_conv_b'], l * 768 + cg * 64, [[1, 64], [1, 1]]), writes=['CWA'])
            S.dma('sp', CWA[64:128, 3:4], AP(W['hy_conv_b'], l * 768 + 512 + cg * 64, [[1, 64], [1, 1]]), writes=['CWA'])
            S.dma('act', CWB[:, 3:4], AP(W['hy_conv_b'], l * 768 + 256 + cg * 64, [[1, 64], [1, 1]]), writes=['CWB'])
            for s in range(NS):
                for hh in range(2):
                    for grp in range(2):
                        np_ = 128 if grp == 0 else 64
                        CWt = CWA if grp == 0 else CWB
                        cwk = 'CWA' if grp == 0 else 'CWB'
                        rows = [(0, 256 + cg * 64), (64, 768 + cg * 64)] if grp == 0 else [(0, 512 + cg * 64)]
                        t0 = s * L + hh * HB
                        for (p0, r0) in rows:
                            if hh == 0:
                                S.dma('sp', Zp[p0:p0 + 64, 1:HB + 2], zT.ap()[r0:r0 + 64, t0:t0 + HB + 1], writes=['Zp'])
                            else:
                                S.dma('sp', Zp[p0:p0 + 64, 0:HB + 1], zT.ap()[r0:r0 + 64, t0 - 1:t0 + HB], writes=['Zp'])
                        if hh == 0:
                            pool(lambda e, np_=np_: e.memset(Zp[0:np_, 0:1], 0.0), (), ['Zp'])
                        else:
                            pool(lambda e, np_=np_: e.memset(Zp[0:np_, HB + 1:HB + 2], 0.0), (), ['Zp'])
                        dve(lambda e, np_=np_, CWt=CWt: e.tensor_scalar(out=T1[0:np_, :], in0=Zp[0:np_, 1:HB + 1], scalar1=CWt[0:np_, 1:2], scalar2=CWt[0:np_, 3:4], op0=ALU.mult, op1=ALU.add),
                            ['Zp', cwk], ['T1'])
                        dve(lambda e, np_=np_, CWt=CWt: e.scalar_tensor_tensor(out=T1[0:np_, :], in0=Zp[0:np_, 0:HB], scalar=CWt[0:np_, 0:1], in1=T1[0:np_, :], op0=ALU.mult, op1=ALU.add),
                            ['Zp', cwk, 'T1'], ['T1'])
                        if grp == 0:
                            ov = OA[0:np_, :].rearrange("p (b j) -> p b j", j=128)[:, :, ::-1]
                        else:
                            ov = OA[0:np_, :].rearrange("p (b j) -> p b j", j=128)
                        dve(lambda e, np_=np_, CWt=CWt, ov=ov: e.scalar_tensor_tensor(out=ov, in0=Zp[0:np_, 2:HB + 2].rearrange("p (b j) -> p b j", j=128), scalar=CWt[0:np_, 2:3],
                                                                                      in1=T1[0:np_, :].rearrange("p (b j) -> p b j", j=128), op0=ALU.mult, op1=ALU.add),
                            ['Zp', cwk, 'T1'], ['OAc'])
                        for bl in range(HB // 128):
                            bq = hh * (HB // 128) + bl
                            u = 4 + ti % 4
                            ti += 1
                            S.op('pe', lambda e, u=u, bl=bl, np_=np_: e.transpose(out=banks[u][:, 0:np_], in_=OA[0:np_, bl * 128:(bl + 1) * 128], identity=ident_f[0:np_, 0:np_]),
                                 ['OAc', 'ident_f'], [bk(u)])
                            if grp == 0:
                                act(lambda e, u=u, bq=bq, s=s: e.activation(out=VTM[:, bq, s, :], in_=banks[u][:, 0:64], func=AF.Copy), [bk(u)], ['VTM'])
                                pool_or = 'dve'
                                dve(lambda e, u=u, bq=bq, s=s: e.tensor_copy(out=X2[:, bq, s, :], in_=banks[u][:, 64:128]), [bk(u)], ['X2TM'])
                            else:
                                act(lambda e, u=u, bq=bq, s=s: e.activation(out=X1[:, bq, s, :], in_=banks[u][:, 0:64], func=AF.Copy), [bk(u)], ['X1TM'])
            _cut = 9
            if _cut <= 1:
                continue
            for cc in range(64 if _cut > 2 else 2):
                c = cg * 64 + cc
                par = cc % 2
                S.dma(('sp', 'act')[par], Hk[par][0][:], AP(KKs, c * KW, [[1, 128], [1, 8064]]), reads=['KKs'], writes=[('Hk', par, 0)])
                S.dma(('pool', 'sp')[par], Hk[par][1][:], AP(KKs, (256 + c) * KW, [[1, 128], [1, 8064]]), reads=['KKs'], writes=[('Hk', par, 1)])
                P1 = banks[par][:, 0:NCOL]
                P2 = banks[2 + par][:, 0:NCOL]
                vv = VTM[:].rearrange("p a s c -> p (a s) c")
                deltas = [0] + [d for d in range(-31, 32) if d != 0]
                for di, dl in enumerate(deltas):
                    a_lo = max(0, dl)
                    a_hi = min(31, 31 + dl)
                    n = a_hi - a_lo + 1
                    b_lo = a_lo - dl
                    S.op('pe', lambda e, par=par, dl=dl, a_lo=a_lo, b_lo=b_lo, n=n, cc=cc, P1=P1, vv=vv, di=di: e.matmul(
                        P1[:, a_lo * NS:(a_lo + n) * NS], lhsT=Hk[par][0][:, 128 * (dl + 31):128 * (dl + 31) + 128], rhs=vv[:, b_lo * NS:(b_lo + n) * NS, cc],
                        start=(di == 0), stop=(di == 62)), [('Hk', par, 0), 'VTM'], [bk(par)])
                x1v = X1[:].rearrange("p a s c -> p (a s) c")[:, :, cc]
                dve(lambda e, par=par, P1=P1, x1v=x1v: e.tensor_tensor(out=U1[par][:], in0=P1, in1=x1v, op=ALU.mult), [bk(par), 'X1TM'], [('U1', par)])
                for di, dl in enumerate(deltas):
                    a_lo = max(0, dl)
                    a_hi = min(31, 31 + dl)
                    n = a_hi - a_lo + 1
                    b_lo = a_lo - dl
                    S.op('pe', lambda e, par=par, dl=dl, a_lo=a_lo, b_lo=b_lo, n=n, P2=P2, di=di: e.matmul(
                        P2[:, a_lo * NS:(a_lo + n) * NS], lhsT=Hk[par][1][:, 128 * (31 - dl):128 * (31 - dl) + 128], rhs=U1[par][:, b_lo * NS:(b_lo + n) * NS],
                        start=(di == 0), stop=(di == 62)), [('Hk', par, 1), ('U1', par)], [bk(2 + par)])
                x2v = X2[:].rearrange("p a s c -> p (a s) c")[:, :, cc]
                y2v = Y2[:].rearrange("p a s c -> p (a s) c")[:, :, cc]
                dve(lambda e, P2=P2, x2v=x2v, y2v=y2v: e.tensor_tensor(out=y2v, in0=P2, in1=x2v, op=ALU.mult), [bk(2 + par), 'X2TM'], ['Y2TM'])
            if _cut <= 3:
                continue
            for s in range(NS):
                for a in range(NA):
                    u = 4 + ti % 4
                    ti += 1
                    S.op('pe', lambda e, u=u, a=a, s=s: e.transpose(out=banks[u][0:64, 0:128], in_=Y2[:, a, s, :], identity=ident_f[:]), ['Y2TM', 'ident_f'], [bk(u)])
                    ovr = OUTF[:, a * 128:(a + 1) * 128][:, ::-1]
                    dve(lambda e, u=u, ovr=ovr: e.tensor_copy(out=ovr, in_=banks[u][0:64, 0:128]), [bk(u)], ['OUTF'])
                S.dma('sp', yT.ap()[256 + cg * 64:256 + (cg + 1) * 64, s * L:(s + 1) * L], OUTF[:], reads=['OUTF'], writes=['yT'])
        S.barrier()


ZRW = 1024
RW_LN_EPS = 64e-5
SCH = 4
VCH = 256


def _rw_helpers(nc, S, st):
    def sb(name, shape, dt=F32):
        return st.enter_context(sbt(nc, name, shape, dt))

    def ps(name, shape, dt=F32):
        return st.enter_context(pst(nc, name, shape, dt))

    def dve(fn, r=(), w=()):
        S.op('dve', fn, r, w)

    def pool(fn, r=(), w=()):
        S.op('pool', fn, r, w)

    def act(fn, r=(), w=()):
        S.op('act', fn, r, w)
    return sb, ps, dve, pool, act


def _rw_shift_load(S, zT, Zp, dst, np_, row0, tok0, hh, HB, C1, C0, pool, dve, zkey, dkey, q='sp'):
    t0 = tok0 + hh * HB
    if hh == 0:
        S.dma(q, Zp[0:np_, 1:HB + 2], zT.ap()[row0:row0 + np_, t0:t0 + HB + 1], writes=[zkey])
        pool(lambda e: e.memset(Zp[0:np_, 0:1], 0.0), (), [zkey])
    else:
        S.dma(q, Zp[0:np_, 0:HB + 1], zT.ap()[row0:row0 + np_, t0 - 1:t0 + HB], writes=[zkey])
        pool(lambda e: e.memset(Zp[0:np_, HB + 1:HB + 2], 0.0), (), [zkey])
    dve(lambda e: e.tensor_scalar_mul(out=dst[0:np_, :], in0=Zp[0:np_, 1:HB + 1], scalar1=C1), [zkey, 'MUC'], [dkey])
    dve(lambda e: e.scalar_tensor_tensor(out=dst[0:np_, :], in0=Zp[0:np_, 0:HB], scalar=C0, in1=dst[0:np_, :], op0=ALU.mult, op1=ALU.add), [zkey, 'MUC', dkey], [dkey])
    dve(lambda e: e.scalar_tensor_tensor(out=dst[0:np_, :], in0=Zp[0:np_, 2:HB + 2], scalar=C0, in1=dst[0:np_, :], op0=ALU.mult, op1=ALU.add), [zkey, 'MUC', dkey], [dkey])


def _rw_load_mu(S, W, l, sb, dve):
    MU = sb("MU", [128, 8]); MC1 = sb("MC1", [128, 8]); MC0 = sb("MC0", [128, 8])
    S.dma_cols('sp', MU, W['rw_mu'], l * 1024, 1, 128, 128, 8, 'MU')
    dve(lambda e: e.tensor_scalar(out=MC1[:], in0=MU[:], scalar1=-1.0, scalar2=1.0, op0=ALU.mult, op1=ALU.add), ['MU'], ['MUC'])
    dve(lambda e: e.tensor_scalar_mul(out=MC0[:], in0=MU[:], scalar1=0.5), ['MU'], ['MUC'])
    return MC1, MC0


def phase_rw_prep(nc, S, W, l, zT, RWS, VSd, NS, C):
    ident_f, eps_ = C['ident_f'], C['epsc']
    HB = 2048
    NF = NS * 4
    with ExitStack() as st:
        sb, ps, dve, pool, act = _rw_helpers(nc, S, st)
        MC1, MC0 = _rw_load_mu(S, W, l, sb, dve)
        def col2(name, wname, off, scale=None):
            t = sb(name, [128, 2])
            S.dma_cols('sp', t, W[wname], off, 1, 128, 128, 2, name)
            return t
        KKc = col2("KKc", 'rw_k_k', l * 256)
        KAc = col2("KAc", 'rw_k_a', l * 256)
        OMK = sb("OMK", [128, 2])
        dve(lambda e: e.tensor_scalar(out=OMK[:], in0=KAc[:], scalar1=-1.0, scalar2=1.0, op0=ALU.mult, op1=ALU.add), ['KAc'], ['OMK'])
        NW0 = [col2("NW0%d" % d, 'rw_w0', (l * 2 + d) * 256) for d in range(2)]
        A0 = [col2("A0%d" % d, 'rw_a0', (l * 2 + d) * 256) for d in range(2)]
        for d in range(2):
            dve(lambda e, d=d: e.tensor_scalar_mul(out=NW0[d][:], in0=NW0[d][:], scalar1=-1.0), ['NW0%d' % d], ['NW0%d' % d])
        W2 = sb("rW2", [64, 2, 256]); A2 = sb("rA2", [128, 2, 256])
        for d in range(2):
            S.dma('sp', W2[:, d, :], W['rw_w2'].ap()[l, d], writes=['rW2'])
            S.dma('act', A2[64:128, d, :], W['rw_a2'].ap()[l, d], writes=['rA2'])
        BLK = sb("rBLK", [128, 128])
        pool(lambda e: e.memset(BLK[:], 0.0), (), ['rBLK'])
        for hh in range(2):
            pool(lambda e, hh=hh: e.memset(BLK[hh * 64:(hh + 1) * 64, hh * 64:(hh + 1) * 64], 1.0), (), ['rBLK'])
        one1 = sb("one1", [128, 1]); nhalf = sb("nhalf", [128, 1])
        pool(lambda e: e.memset(one1[:], 1.0), (), ['one1'])
        pool(lambda e: e.memset(nhalf[:], -0.5), (), ['nhalf'])

        Zp = sb("rZp", [128, HB + 2])
        Rt = sb("Rt", [128, HB]); Kt = sb("Kt", [128, HB]); Vt = sb("Vt", [128, HB]); LW = sb("LW", [128, HB])
        KKn = sb("KKn", [128, HB]); An = sb("An", [128, HB]); Rr_ = sb("Rrev", [128, HB]); Ar_ = sb("Arev", [128, HB]); Vr_ = sb("Vrev", [128, HB])
        DEC = sb("DEC", [128, HB]); AA = sb("AA", [128, HB]); KD = sb("KD", [128, HB]); BBt = sb("BBt", [128, HB])
        tmp = [sb("rtmp%d" % i, [128, 512]) for i in range(2)]
        TM = [sb("TM%d" % i, [128, 2, 5, 64]) for i in range(3)]
        banks = [ps("rpb%d" % i, [128, 512]) for i in range(8)]

        def bk(i):
            return ('rpb', i)
        ti = 0
        tmi = 0
        for s in range(NS):
            for ch2 in range(2):
                _rw_shift_load(S, zT, Zp, LW, 128, ZRW + 768, s * L, ch2, HB, MC1[:, 6:7], MC0[:, 6:7], pool, dve, 'rZp', 'LW')
                act(lambda e: e.activation(out=LW[0:64, :], in_=LW[0:64, :], func=AF.Tanh), ['LW'], ['LW'])
                for hp in range(2):
                    _rw_shift_load(S, zT, Zp, Rt, 128, ZRW + hp * 128, s * L, ch2, HB, MC1[:, hp:hp + 1], MC0[:, hp:hp + 1], pool, dve, 'rZp', 'Rt', 'sp')
                    _rw_shift_load(S, zT, Zp, Kt, 128, ZRW + 256 + hp * 128, s * L, ch2, HB, MC1[:, 2 + hp:3 + hp], MC0[:, 2 + hp:3 + hp], pool, dve, 'rZp', 'Kt', 'act')
                    _rw_shift_load(S, zT, Zp, Vt, 128, ZRW + 512 + hp * 128, s * L, ch2, HB, MC1[:, 4 + hp:5 + hp], MC0[:, 4 + hp:5 + hp], pool, dve, 'rZp', 'Vt', 'sp')
                    dve(lambda e, hp=hp: e.tensor_scalar_mul(out=KKn[:], in0=Kt[:], scalar1=KKc[:, hp:hp + 1]), ['Kt', 'KKc'], ['KKn'])
                    for tb in range(HB // 512):
                        cs = slice(tb * 512, (tb + 1) * 512)
                        u = ti % 2
                        ti += 1
                        pool(lambda e, u=u, cs=cs: e.tensor_tensor(out=tmp[u][:], in0=KKn[:, cs], in1=KKn[:, cs], op=ALU.mult), ['KKn'], [('rtmp', u)])
                        S.op('pe', lambda e, u=u: e.matmul(banks[u][:], lhsT=BLK[:], rhs=tmp[u][:], start=True, stop=True), [('rtmp', u), 'rBLK'], [bk(u)])
                        act(lambda e, u=u: e.activation(out=tmp[u][:], in_=banks[u][:], func=AF.Sqrt), [bk(u)], [('rtmp', u)])
                        dve(lambda e, u=u: e.tensor_scalar_max(out=tmp[u][:], in0=tmp[u][:], scalar1=1e-12), [('rtmp', u)], [('rtmp', u)])
                        dve(lambda e, u=u: e.reciprocal(out=tmp[u][:], in_=tmp[u][:]), [('rtmp', u)], [('rtmp', u)])
                        dve(lambda e, u=u, cs=cs: e.tensor_tensor(out=KKn[:, cs], in0=KKn[:, cs], in1=tmp[u][:], op=ALU.mult), ['KKn', ('rtmp', u)], ['KKn'])
                    dve(lambda e: e.tensor_scalar_mul(out=An[:], in0=KKn[:], scalar1=-1.0), ['KKn'], ['An'])
                    rv3 = lambda t_: t_[:].rearrange("p (b j) -> p b j", j=128)[:, :, ::-1]
                    n3 = lambda t_: t_[:].rearrange("p (b j) -> p b j", j=128)
                    dve(lambda e: e.tensor_copy(out=rv3(Rr_), in_=n3(Rt)), ['Rt'], ['Rrev'])
                    dve(lambda e: e.tensor_copy(out=rv3(Ar_), in_=n3(An)), ['An'], ['Arev'])
                    f0 = (s * 2 + 0) * 2 + hp
                    f1 = (s * 2 + 1) * 2 + hp
                    S.dma('pool', AP(VSd, f0 * L + ch2 * HB, [[NF * L, 128], [1, HB]]), Vt[:], reads=['Vt'], writes=['VSd'])
                    dve(lambda e: e.tensor_copy(out=Vr_[:, ::-1], in_=Vt[:]), ['Vt'], ['Vrev'])
                    S.dma('pool', AP(VSd, f1 * L + (1 - ch2) * HB, [[NF * L, 128], [1, HB]]), Vr_[:], reads=['Vrev'], writes=['VSd'])
                    for d in range(2):
                        f = (s * 2 + d) * 2 + hp
                        ov = (lambda t_, cs: t_[:, cs]) if d == 0 else (lambda t_, cs: t_[:, cs].rearrange("p (b j) -> p b j", j=128)[:, :, ::-1])
                        iv = (lambda a_: a_) if d == 0 else (lambda a_: a_.rearrange("p (b j) -> p b j", j=128))
                        for tb in range(HB // 512):
                            cs = slice(tb * 512, (tb + 1) * 512)
                            u = 2 + ti % 2
                            ti += 1
                            S.op('pe', lambda e, u=u, cs=cs, d=d, hp=hp: e.matmul(banks[u][:], lhsT=W2[:, d, hp * 128:(hp + 1) * 128], rhs=LW[0:64, cs], start=True, stop=True),
                                 ['rW2', 'LW'], [bk(u)])
                            w_ = u - 2
                            act(lambda e, u=u, w_=w_, d=d, hp=hp: e.activation(out=tmp[w_][:], in_=banks[u][:], func=AF.Exp, scale=-1.0, bias=NW0[d][:, hp:hp + 1]), [bk(u), 'NW0%d' % d], [('rtmp', w_)])
                            act(lambda e, w_=w_: e.activation(out=tmp[w_][:], in_=tmp[w_][:], func=AF.Ln, bias=one1[:, 0:1]), [('rtmp', w_), 'one1'], [('rtmp', w_)])
                            act(lambda e, w_=w_: e.activation(out=tmp[w_][:], in_=tmp[w_][:], func=AF.Exp, scale=-1.0, bias=nhalf[:, 0:1]), [('rtmp', w_), 'nhalf'], [('rtmp', w_)])
                            act(lambda e, w_=w_, cs=cs, ov=ov: e.activation(out=DEC[:, cs], in_=tmp[w_][:], func=AF.Exp, scale=-1.0), [('rtmp', w_)], ['DEC'])
                            u2 = 4 + ti % 2
                            S.op('pe', lambda e, u2=u2, cs=cs, d=d, hp=hp: e.matmul(banks[u2][:], lhsT=A2[64:128, d, hp * 128:(hp + 1) * 128], rhs=LW[64:128, cs], start=True, stop=True),
                                 ['rA2', 'LW'], [bk(u2)])
                            act(lambda e, u2=u2, cs=cs, d=d, hp=hp: e.activation(out=AA[:, cs], in_=banks[u2][:], func=AF.Sigmoid, bias=A0[d][:, hp:hp + 1]), [bk(u2), 'A0%d' % d], ['AA'])
                        dve(lambda e, hp=hp: e.tensor_scalar(out=KD[:], in0=AA[:], scalar1=KAc[:, hp:hp + 1], scalar2=OMK[:, hp:hp + 1], op0=ALU.mult, op1=ALU.add), ['AA', 'KAc', 'OMK'], ['KD'])
                        dve(lambda e: e.tensor_tensor(out=KD[:], in0=KD[:], in1=Kt[:], op=ALU.mult), ['KD', 'Kt'], ['KD'])
                        dve(lambda e: e.tensor_tensor(out=BBt[:], in0=KKn[:], in1=AA[:], op=ALU.mult), ['KKn', 'AA'], ['BBt'])
                        if d == 1:
                            for (src, k_) in ((DEC, 'DEC'), (KD, 'KD'), (BBt, 'BBt')):
                                dve(lambda e, src=src: e.tensor_copy(out=rv3(AA), in_=n3(src)), [k_], ['AA'])
                                pool(lambda e, src=src: e.tensor_copy(out=src[:], in_=AA[:]), ['AA'], [k_])
                        srcs = [(An if d == 0 else Ar_, 'An' if d == 0 else 'Arev'), (DEC, 'DEC'), (BBt, 'BBt'), (KD, 'KD'), (Rt if d == 0 else Rr_, 'Rt' if d == 0 else 'Rrev')]
                        for bl in range(HB // 128):
                            bq = ch2 * (HB // 128) + bl
                            tq = bq if d == 0 else 31 - bq
                            tm = tmi % 3
                            tmi += 1
                            for j, (src, k_) in enumerate(srcs):
                                u = 6 + ti % 2
                                ti += 1
                                S.op('pe', lambda e, u=u, src=src, bl=bl: e.transpose(out=banks[u][:, 0:128], in_=src[:, bl * 128:(bl + 1) * 128], identity=ident_f[:]),
                                     [k_, 'ident_f'], [bk(u)])
                                if j % 2:
                                    act(lambda e, u=u, tm=tm, j=j: e.activation(out=TM[tm][:, :, j, :], in_=banks[u][:, 0:128].rearrange("p (h c) -> p h c", c=64), func=AF.Copy), [bk(u)], [('TM', tm)])
                                else:
                                    dve(lambda e, u=u, tm=tm, j=j: e.tensor_copy(out=TM[tm][:, :, j, :], in_=banks[u][:, 0:128].rearrange("p (h c) -> p h c", c=64)), [bk(u)], [('TM', tm)])
                            for hh in range(2):
                                S.dma(('sp', 'act')[hh], AP(RWS, ((hh * NF + f) * L + tq * 128) * 320, [[320, 128], [1, 320]]), TM[tm][:, hh, :, :].rearrange("p a b -> p (a b)"),
                                      reads=[('TM', tm)], writes=['RWS'])
        S.barrier()


def phase_rw_scan(nc, S, RWS, VSd, YSd, NS):
    NF = NS * 4
    with ExitStack() as st:
        sb, ps, dve, pool, act = _rw_helpers(nc, S, st)
        OPB = [sb("OPB%d" % i, [128, NF, SCH, 5, 64]) for i in range(2)]
        VS = [sb("VS%d" % i, [128, NF, VCH]) for i in range(2)]
        YC = [sb("YC%d" % i, [128, NF, VCH]) for i in range(2)]
        St = sb("Sst", [128, NF, 64])
        T1 = sb("sT1", [128, NF, 64]); T2 = sb("sT2", [128, NF, 64]); T3 = sb("sT3", [128, NF, 64])
        KV = [sb("sKV%d" % i, [128, NF, 64]) for i in range(2)]
        SA = sb("sSA", [128, NF])
        pool(lambda e: e.memset(St[:], 0.0), (), ['Sst'])
        nsteps = L
        for t in range(nsteps):
            if t % VCH == 0:
                vb = (t // VCH) % 2
                S.dma('pool', VS[vb][:], AP(VSd, t, [[NF * L, 128], [L, NF], [1, VCH]]), reads=['VSd'], writes=[('VS', vb)])
            if t % SCH == 0:
                ob = (t // SCH) % 2
                for hh in range(2):
                    S.dma(('sp', 'act')[hh], OPB[ob][hh * 64:(hh + 1) * 64, :, :, :, :].rearrange("p f c j k -> p f (c j k)"),
                          AP(RWS, (hh * NF * L + t) * 320, [[0, 64], [L * 320, NF], [1, SCH * 320]]), reads=['RWS'], writes=[('OPB', ob)])
            ob = (t // SCH) % 2
            c = t % SCH
            vb = (t // VCH) % 2
            vc = t % VCH
            A_ = OPB[ob][:, :, c, 0, :]; W_ = OPB[ob][:, :, c, 1, :]; B_ = OPB[ob][:, :, c, 2, :]; K_ = OPB[ob][:, :, c, 3, :]; R_ = OPB[ob][:, :, c, 4, :]
            kvb = t % 2
            vcol = VS[vb][:, :, vc:vc + 1].to_broadcast([128, NF, 64])
            pool(lambda e, kvb=kvb, K_=K_, vcol=vcol: e.tensor_tensor(out=KV[kvb][:], in0=K_, in1=vcol, op=ALU.mult), [('OPB', ob), ('VS', vb)], [('sKV', kvb)])
            dve(lambda e, A_=A_: e.tensor_tensor(out=T1[:], in0=St[:], in1=A_, op=ALU.mult), ['Sst', ('OPB', ob)], ['sT1'])
            dve(lambda e: e.tensor_reduce(out=SA[:], in_=T1[:], axis=AX.X, op=ALU.add), ['sT1'], ['sSA'])
            dve(lambda e, B_=B_: e.tensor_tensor(out=T2[:], in0=B_, in1=SA[:].unsqueeze(2).to_broadcast([128, NF, 64]), op=ALU.mult), [('OPB', ob), 'sSA'], ['sT2'])
            dve(lambda e, W_=W_: e.tensor_tensor(out=St[:], in0=St[:], in1=W_, op=ALU.mult), ['Sst', ('OPB', ob)], ['Sst'])
            dve(lambda e: e.tensor_tensor(out=St[:], in0=St[:], in1=T2[:], op=ALU.add), ['Sst', 'sT2'], ['Sst'])
            dve(lambda e, kvb=kvb: e.tensor_tensor(out=St[:], in0=St[:], in1=KV[kvb][:], op=ALU.add), ['Sst', ('sKV', kvb)], ['Sst'])
            dve(lambda e, R_=R_: e.tensor_tensor(out=T3[:], in0=St[:], in1=R_, op=ALU.mult), ['Sst', ('OPB', ob)], ['sT3'])
            dve(lambda e, vb=vb, vc=vc: e.tensor_reduce(out=YC[vb][:, :, vc], in_=T3[:], axis=AX.X, op=ALU.add), ['sT3'], [('YC', vb)])
            if vc == VCH - 1:
                S.dma('pool', AP(YSd, t - (VCH - 1), [[NF * L, 128], [L, NF], [1, VCH]]), YC[vb][:], reads=[('YC', vb)], writes=['YSd'])
        S.barrier()


def phase_rw_post(nc, S, W, l, zT, yT, YSd, NS, C):
    eps_ = C['epsc']
    HB = 2048
    NF = NS * 4
    with ExitStack() as st:
        sb, ps, dve, pool, act = _rw_helpers(nc, S, st)
        MC1, MC0 = _rw_load_mu(S, W, l, sb, dve)

        def col2(name, wname, off):
            t = sb(name, [128, 2])
            S.dma_cols('sp', t, W[wname], off, 1, 128, 128, 2, name)
            return t
        LNW = col2("LNW", 'rw_ln_w', l * 256); LNB = col2("LNB", 'rw_ln_b', l * 256); RKc = col2("RKc", 'rw_r_k', l * 256)
        G2 = sb("rG2", [128, 256])
        S.dma('sp', G2[:], W['rw_g2'].ap()[l], writes=['rG2'])
        BLK = sb("pBLK", [128, 128])
        pool(lambda e: e.memset(BLK[:], 0.0), (), ['pBLK'])
        for hh in range(2):
            pool(lambda e, hh=hh: e.memset(BLK[hh * 64:(hh + 1) * 64, hh * 64:(hh + 1) * 64], 1.0 / 64), (), ['pBLK'])
        lneps = sb("lneps", [128, 1])
        pool(lambda e: e.memset(lneps[:], RW_LN_EPS), (), ['lneps'])
        Zp = sb("pZp", [128, HB + 2])
        Rt = sb("pRt", [128, HB]); Kt = sb("pKt", [128, HB]); Vt = sb("pVt", [128, HB]); Gd = sb("pGd", [128, HB])
        Yf = sb("pYf", [128, HB]); Yb = sb("pYb", [128, HB]); Gt = sb("pGt", [128, HB])
        tmp = [sb("ptmp%d" % i, [128, 512]) for i in range(2)]
        tmq = [sb("ptmq%d" % i, [128, 512]) for i in range(2)]
        banks = [ps("ppb%d" % i, [128, 512]) for i in range(6)]

        def bk(i):
            return ('ppb', i)
        ti = 0
        for s in range(NS):
            for ch2 in range(2):
                _rw_shift_load(S, zT, Zp, Gd, 128, ZRW + 896, s * L, ch2, HB, MC1[:, 7:8], MC0[:, 7:8], pool, dve, 'pZp', 'pGd')
                act(lambda e: e.activation(out=Gd[:], in_=Gd[:], func=AF.Sigmoid), ['pGd'], ['pGd'])
                for hp in range(2):
                    _rw_shift_load(S, zT, Zp, Rt, 128, ZRW + hp * 128, s * L, ch2, HB, MC1[:, hp:hp + 1], MC0[:, hp:hp + 1], pool, dve, 'pZp', 'pRt', 'sp')
                    _rw_shift_load(S, zT, Zp, Kt, 128, ZRW + 256 + hp * 128, s * L, ch2, HB, MC1[:, 2 + hp:3 + hp], MC0[:, 2 + hp:3 + hp], pool, dve, 'pZp', 'pKt', 'act')
                    _rw_shift_load(S, zT, Zp, Vt, 128, ZRW + 512 + hp * 128, s * L, ch2, HB, MC1[:, 4 + hp:5 + hp], MC0[:, 4 + hp:5 + hp], pool, dve, 'pZp', 'pVt', 'sp')
                    f0 = (s * 2 + 0) * 2 + hp
                    f1 = (s * 2 + 1) * 2 + hp
                    S.dma('sp', Yf[:], AP(YSd, f0 * L + ch2 * HB, [[NF * L, 128], [1, HB]]), reads=['YSd'], writes=['pYf'])
                    S.dma('act', Yb[:], AP(YSd, f1 * L + (1 - ch2) * HB, [[NF * L, 128], [1, HB]]), reads=['YSd'], writes=['pYb'])
                    dve(lambda e: e.tensor_tensor(out=Yf[:], in0=Yf[:], in1=Yb[:, ::-1], op=ALU.add), ['pYf', 'pYb'], ['pYf'])
                    dve(lambda e, hp=hp: e.scalar_tensor_tensor(out=Kt[:], in0=Kt[:], scalar=RKc[:, hp:hp + 1], in1=Rt[:], op0=ALU.mult, op1=ALU.mult), ['pKt', 'RKc', 'pRt'], ['pKt'])
                    for tb in range(HB // 512):
                        cs = slice(tb * 512, (tb + 1) * 512)
                        u = ti % 2
                        ti += 1
                        S.op('pe', lambda e, u=u, cs=cs: e.matmul(banks[u][:], lhsT=BLK[:], rhs=Yf[:, cs], start=True, stop=True), ['pYf', 'pBLK'], [bk(u)])
                        dve(lambda e, u=u, cs=cs: e.tensor_tensor(out=tmp[u][:], in0=Yf[:, cs], in1=banks[u][:], op=ALU.subtract), ['pYf', bk(u)], [('ptmp', u)])
                        pool(lambda e, u=u: e.tensor_tensor(out=tmq[u][:], in0=tmp[u][:], in1=tmp[u][:], op=ALU.mult), [('ptmp', u)], [('ptmq', u)])
                        S.op('pe', lambda e, u=u: e.matmul(banks[2 + u][:], lhsT=BLK[:], rhs=tmq[u][:], start=True, stop=True), [('ptmq', u), 'pBLK'], [bk(2 + u)])
                        act(lambda e, u=u: e.activation(out=tmq[u][:], in_=banks[2 + u][:], func=AF.Sqrt, bias=lneps[:, 0:1]), [bk(2 + u), 'lneps'], [('ptmq', u)])
                        dve(lambda e, u=u: e.reciprocal(out=tmq[u][:], in_=tmq[u][:]), [('ptmq', u)], [('ptmq', u)])
                        dve(lambda e, u=u: e.tensor_tensor(out=tmp[u][:], in0=tmp[u][:], in1=tmq[u][:], op=ALU.mult), [('ptmp', u), ('ptmq', u)], [('ptmp', u)])
                        dve(lambda e, u=u, hp=hp: e.tensor_scalar(out=tmp[u][:], in0=tmp[u][:], scalar1=LNW[:, hp:hp + 1], scalar2=LNB[:, hp:hp + 1], op0=ALU.mult, op1=ALU.add),
                            [('ptmp', u), 'LNW', 'LNB'], [('ptmp', u)])
                        S.op('pe', lambda e, u=u, cs=cs: e.matmul(banks[4 + u][:], lhsT=BLK[:], rhs=Kt[:, cs], start=True, stop=True), ['pKt', 'pBLK'], [bk(4 + u)])
                        dve(lambda e, u=u, cs=cs: e.scalar_tensor_tensor(out=tmq[u][:], in0=banks[4 + u][:], scalar=64.0, in1=Vt[:, cs], op0=ALU.mult, op1=ALU.mult), [bk(4 + u), 'pVt'], [('ptmq', u)])
                        dve(lambda e, u=u: e.tensor_tensor(out=tmp[u][:], in0=tmp[u][:], in1=tmq[u][:], op=ALU.add), [('ptmp', u), ('ptmq', u)], [('ptmp', u)])
                        S.op('pe', lambda e, u=u, cs=cs, hp=hp: e.matmul(banks[u][:], lhsT=G2[:, hp * 128:(hp + 1) * 128], rhs=Gd[:, cs], start=True, stop=True), ['rG2', 'pGd'], [bk(u)])
                        dve(lambda e, u=u, cs=cs: e.tensor_tensor(out=Gt[:, cs], in0=tmp[u][:], in1=banks[u][:], op=ALU.mult), [('ptmp', u), bk(u)], ['pGt'])
                    S.dma('pool', yT.ap()[512 + hp * 128:512 + (hp + 1) * 128, s * L + ch2 * HB:s * L + (ch2 + 1) * HB], Gt[:], reads=['pGt'], writes=['yT'])
        S.barrier()


_STATIC = {}


def static_inputs():
    if not _STATIC:
        m01, mneg = na_static_masks()
        _STATIC['na_m01'] = m01
        _STATIC['na_mneg'] = mneg
        _STATIC['hy_mul'] = np.array([1.0] + list(range(1, 9)) + list(range(1, 9)), np.float32).reshape(17, 1)
        _STATIC['hy_ph'] = np.array([0.0] + [0.0] * 8 + [0.25] * 8, np.float32).reshape(17, 1)
    return dict(_STATIC)


def shard_inputs(inputs, NS=3):
    xs = [inputs['x_prompt'][i] for i in range(inputs['x_prompt'].shape[0])]
    xs += [inputs['x_sample'][i] for i in range(inputs['x_sample'].shape[0])]
    nreal = len(xs)
    while len(xs) < NCORES * NS:
        xs.append(xs[len(xs) - nreal])
    maps = []
    for c in range(NCORES):
        m = {n: np.ascontiguousarray(inputs[n], dtype=np.float32) for n in WNAMES}
        m['x'] = np.ascontiguousarray(np.concatenate(xs[c * NS:(c + 1) * NS], axis=0), dtype=np.float32)
        m.update(static_inputs())
        maps.append(m)
    return maps, nreal


FUSED = False


def kernel(**inputs):
    NS = 3
    if FUSED:
        wshapes = {n: inputs[n].shape for n in WNAMES}
        nc = build(wshapes, NL=DEPTH, NS=NS, mixers=IMPLEMENTED)
        maps, nreal = shard_inputs(inputs, NS)
        res = run_bass_kernel_spmd(nc, maps, core_ids=list(range(NCORES)))
        ys = [np.asarray(res.results[c]['y'], dtype=np.float32) for c in range(NCORES)]
    else:
        wshapes = {n: (1,) + tuple(inputs[n].shape[1:]) for n in WNAMES}
        nc = build(wshapes, NL=1, NS=NS, mixers=IMPLEMENTED)
        maps, nreal = shard_inputs(inputs, NS)
        xs = [m['x'] for m in maps]
        stat = static_inputs()
        for l in range(DEPTH):
            lw = {n: np.ascontiguousarray(inputs[n][l:l + 1], dtype=np.float32) for n in WNAMES}
            lmaps = []
            for c in range(NCORES):
                m = dict(lw)
                m['x'] = xs[c]
                m.update(stat)
                lmaps.append(m)
            res = run_bass_kernel_spmd(nc, lmaps, core_ids=list(range(NCORES)))
            xs = [np.ascontiguousarray(np.asarray(res.results[c]['y'], dtype=np.float32)) for c in range(NCORES)]
        ys = xs
    seqs = []
    for c in range(NCORES):
        y = ys[c].reshape(NS, L, D)
        for s in range(NS):
            seqs.append(y[s])
    seqs = seqs[:nreal]
    nb = inputs['x_prompt'].shape[0]
    y_prompt = np.stack(seqs[:nb], axis=0)
    y_sample = np.stack(seqs[nb:], axis=0)
    return (y_prompt, y_sample)
```
